# Optimizing a Trainium2 kernel written in Bass

```python
import math
import jax, jax.numpy as jnp
from jax import lax
import numpy as np

D_MODEL = 1024
BATCH = 8
SEQ = 2048
DEPTH = 4
DEC_BATCH = 128
DEC_SEQ = 4
PAST_LEN = 16384
PAGE_SIZE = 128

N_MIXERS = 2
N_A = (DEPTH + 1) // 2
N_B = DEPTH // 2
H_A = 8
DV_A = D_MODEL // H_A
DK_A = DV_A // 2
QK_A = H_A * DK_A
VW_A = H_A * DV_A
A_IN = 2 * QK_A + 2 * VW_A + 2 * H_A
GATE_CAP = 15.0
HK_B = 8
HV_B = 16
DK_B = 128
DV_B = 128
QK_B = HK_B * DK_B
VW_B = HV_B * DV_B
CONV_W = 4
CONV_DIM = 2 * QK_B + VW_B
B_IN = CONV_DIM + VW_B + 2 * HV_B
CHUNK = 64
D_FF = 4 * D_MODEL
ALPHA = (2.0 * DEPTH) ** 0.25
BETA_INIT = (8.0 * DEPTH) ** -0.25
LN_EPS = 1e-5
RMS_EPS = 1e-6

kernel_name = "hybrid_mlstm_gdn_decode_step"


def layer_norm(x, g, b):
    xf = x.astype(jnp.float32)
    mu = xf.mean(-1, keepdims=True)
    var = jnp.square(xf - mu).mean(-1, keepdims=True)
    return ((xf - mu) * lax.rsqrt(var + LN_EPS) * g.astype(jnp.float32) + b.astype(jnp.float32)).astype(x.dtype)


def head_rms(h):
    return h * lax.rsqrt(jnp.mean(jnp.square(h), -1, keepdims=True) + RMS_EPS)


def l2norm(x):
    return x * lax.rsqrt(jnp.sum(jnp.square(x), -1, keepdims=True) + RMS_EPS)


def to_chunks(a, L):
    B, T = a.shape[:2]
    a = a.reshape((B, T // L, L) + a.shape[2:])
    return jnp.swapaxes(jnp.moveaxis(a, 1, 0), 2, 3)


def from_chunks(a):
    a = jnp.swapaxes(jnp.moveaxis(a, 0, 1), 2, 3)
    return a.reshape((a.shape[0], a.shape[1] * a.shape[2]) + a.shape[3:])


def mlstm_chunked(q, k, v, i_pre, logf, C0, n0, m0):
    T = q.shape[1]
    L = math.gcd(T, CHUNK)
    causal = jnp.tril(jnp.ones((L, L), dtype=bool))
    xs = (to_chunks(q, L), to_chunks(k, L), to_chunks(v, L), to_chunks(i_pre, L), to_chunks(logf, L))

    def step(carry, xc):
        C, n, m = carry
        qc, kc, vc, ic, fc = xc
        b = jnp.cumsum(fc, axis=-1)
        D = jnp.where(causal, b[..., :, None] - b[..., None, :] + ic[..., None, :], -jnp.inf)
        inter = b + m[..., None]
        m_t = jnp.maximum(inter, D.max(-1))
        w_intra = jnp.exp(D - m_t[..., None])
        w_inter = jnp.exp(inter - m_t)
        s = jnp.einsum('bhtk,bhsk->bhts', qc, kc) * w_intra
        num = jnp.einsum('bhts,bhsv->bhtv', s, vc) + w_inter[..., None] * jnp.einsum('bhtk,bhkv->bhtv', qc, C)
        den = s.sum(-1) + w_inter * jnp.einsum('bhtk,bhk->bht', qc, n)
        h = num / jnp.maximum(jnp.abs(den), jnp.exp(-m_t))[..., None]
        m_new = m_t[..., -1]
        w_k = jnp.exp(b[..., -1:] - b + ic - m_new[..., None])
        decay = jnp.exp(b[..., -1] + m - m_new)
        C_new = decay[..., None, None] * C + jnp.einsum('bhs,bhsk,bhsv->bhkv', w_k, kc, vc)
        n_new = decay[..., None] * n + jnp.einsum('bhs,bhsk->bhk', w_k, kc)
        return (C_new, n_new, m_new), h

    (C, n, m), h = lax.scan(step, (C0, n0, m0), xs)
    return from_chunks(h), C, n, m


def gated_delta_chunked(q, k, v, g, beta, S0):
    T = q.shape[1]
    L = math.gcd(T, CHUNK)
    qc, kc, vc, gc, bc = (to_chunks(a, L) for a in (q, k, v, g, beta))
    G = jnp.cumsum(gc, axis=-1)
    tril = jnp.tril(jnp.ones((L, L), dtype=bool))
    strict = jnp.tril(jnp.ones((L, L), dtype=bool), -1)
    decay = jnp.exp(jnp.where(tril, G[..., :, None] - G[..., None, :], -jnp.inf))
    kb = kc * bc[..., None]
    M = jnp.where(strict, jnp.einsum('...tk,...sk->...ts', kb, kc) * decay, 0.0)
    A = M + jnp.eye(L, dtype=M.dtype)
    rhs = jnp.concatenate([vc * bc[..., None], kb * jnp.exp(G)[..., None]], axis=-1)
    sol = lax.linalg.triangular_solve(A, rhs, left_side=True, lower=True)
    u, w = sol[..., :DV_B], sol[..., DV_B:]
    attn = jnp.einsum('...tk,...sk->...ts', qc, kc) * decay
    qg = qc * jnp.exp(G)[..., None]
    kdec = kc * jnp.exp(G[..., -1:] - G)[..., None]
    gL = jnp.exp(G[..., -1])

    def step(S, xc):
        u_c, w_c, attn_c, qg_c, kdec_c, gL_c = xc
        v_new = u_c - jnp.einsum('bhtk,bhkv->bhtv', w_c, S)
        o = jnp.einsum('bhtk,bhkv->bhtv', qg_c, S) + jnp.einsum('bhts,bhsv->bhtv', attn_c, v_new)
        S = gL_c[..., None, None] * S + jnp.einsum('bhsk,bhsv->bhkv', kdec_c, v_new)
        return S, o

    S, o = lax.scan(step, S0, (u, w, attn, qg, kdec, gL))
    return from_chunks(o), S


def mlstm_mixer(x, w_in, gate_b, norm_w, w_out, C0, n0, m0):
    B, T, _ = x.shape
    f32 = jnp.float32
    proj = x @ w_in
    q = proj[..., :QK_A].reshape(B, T, H_A, DK_A).astype(f32)
    k = proj[..., QK_A:2 * QK_A].reshape(B, T, H_A, DK_A).astype(f32) * (DK_A ** -0.5)
    v = proj[..., 2 * QK_A:2 * QK_A + VW_A].reshape(B, T, H_A, DV_A).astype(f32)
    o = proj[..., 2 * QK_A + VW_A:2 * QK_A + 2 * VW_A].astype(f32)
    gates = proj[..., 2 * QK_A + 2 * VW_A:].astype(f32) + gate_b.astype(f32)
    gates = GATE_CAP * jnp.tanh(gates / GATE_CAP)
    i_pre = gates[..., :H_A]
    logf = jax.nn.log_sigmoid(gates[..., H_A:])
    h, C, n, m = mlstm_chunked(q, k, v, i_pre, logf, C0.astype(f32), n0.astype(f32), m0.astype(f32))
    h = head_rms(h) * norm_w.astype(f32).reshape(H_A, DV_A)
    h = jax.nn.sigmoid(o) * h.reshape(B, T, VW_A)
    return h.astype(x.dtype) @ w_out, C, n, m


def gdn_mixer(x, w_in, conv_w, dt_bias, a_log, norm_w, w_out, S0, conv0):
    B, T, _ = x.shape
    f32 = jnp.float32
    proj = x @ w_in
    qkv = proj[..., :CONV_DIM]
    z = proj[..., CONV_DIM:CONV_DIM + VW_B].astype(f32)
    b = proj[..., CONV_DIM + VW_B:CONV_DIM + VW_B + HV_B].astype(f32)
    a = proj[..., CONV_DIM + VW_B + HV_B:].astype(f32)
    xp = jnp.concatenate([conv0.astype(qkv.dtype), qkv], axis=1)
    c = xp[:, 0:T] * conv_w[0]
    for j in range(1, CONV_W):
        c = c + xp[:, j:j + T] * conv_w[j]
    c = jax.nn.silu(c.astype(f32))
    new_conv = xp[:, T:]
    rep = HV_B // HK_B
    q = l2norm(c[..., :QK_B].reshape(B, T, HK_B, DK_B)) * (DK_B ** -0.5)
    k = l2norm(c[..., QK_B:2 * QK_B].reshape(B, T, HK_B, DK_B))
    v = c[..., 2 * QK_B:].reshape(B, T, HV_B, DV_B)
    q = jnp.repeat(q, rep, axis=2)
    k = jnp.repeat(k, rep, axis=2)
    beta = jax.nn.sigmoid(b)
    g = -jnp.exp(a_log.astype(f32)) * jax.nn.softplus(a + dt_bias.astype(f32))
    o, S = gated_delta_chunked(q, k, v, g, beta, S0.astype(f32))
    o = head_rms(o) * norm_w.astype(f32).reshape(HV_B, DV_B)
    o = o.reshape(B, T, VW_B) * jax.nn.silu(z)
    return o.astype(x.dtype) @ w_out, S, new_conv


def sq_relu_mlp(x, w1, w2):
    return jnp.square(jax.nn.relu(x @ w1)) @ w2


def trunk(x, C, n, m, S, conv, a_w_in, a_gate_b, a_norm_w, a_w_out, b_w_in, b_conv_w, b_dt_bias,
          b_a_log, b_norm_w, b_w_out, mlp_w1, mlp_w2, ln1_g, ln1_b, ln2_g, ln2_b):
    new_C, new_n, new_m, new_S, new_conv = [], [], [], [], []
    for layer in range(DEPTH):
        j = layer // N_MIXERS
        if layer % N_MIXERS == 0:
            y, Cj, nj, mj = mlstm_mixer(x, a_w_in[j], a_gate_b[j], a_norm_w[j], a_w_out[j], C[j], n[j], m[j])
            new_C.append(Cj.astype(C.dtype)); new_n.append(nj.astype(n.dtype)); new_m.append(mj.astype(m.dtype))
        else:
            y, Sj, cj = gdn_mixer(x, b_w_in[j], b_conv_w[j], b_dt_bias[j], b_a_log[j], b_norm_w[j], b_w_out[j],
                                  S[j], conv[j])
            new_S.append(Sj.astype(S.dtype)); new_conv.append(cj.astype(conv.dtype))
        x = layer_norm(ALPHA * x + y, ln1_g[layer], ln1_b[layer])
        x = layer_norm(ALPHA * x + sq_relu_mlp(x, mlp_w1[layer], mlp_w2[layer]), ln2_g[layer], ln2_b[layer])
    return x, jnp.stack(new_C), jnp.stack(new_n), jnp.stack(new_m), jnp.stack(new_S), jnp.stack(new_conv)


def setup_inputs(seed: int = 0) -> dict:
    key = jax.random.key(seed)
    ks = jax.random.split(key, 24)
    nrm = jax.random.normal
    f32 = jnp.float32
    x_prompt = nrm(ks[0], (BATCH, SEQ, D_MODEL), f32)
    x_sample = nrm(ks[1], (DEC_BATCH, DEC_SEQ, D_MODEL), f32)
    state_mlstm_C = nrm(ks[2], (N_A, DEC_BATCH, H_A, DK_A, DV_A), f32)
    state_mlstm_n = nrm(ks[3], (N_A, DEC_BATCH, H_A, DK_A), f32)
    state_mlstm_m = nrm(ks[4], (N_A, DEC_BATCH, H_A), f32)
    state_gdn_S = nrm(ks[5], (N_B, DEC_BATCH, HV_B, DK_B, DV_B), f32)
    state_gdn_conv = nrm(ks[6], (N_B, DEC_BATCH, CONV_W - 1, CONV_DIM), f32)
    a_scale = jnp.concatenate([jnp.ones((2 * QK_A,), f32), jnp.full((VW_A,), BETA_INIT, f32),
                               jnp.ones((VW_A + 2 * H_A,), f32)])
    a_w_in = nrm(ks[7], (N_A, D_MODEL, A_IN), f32) * (D_MODEL ** -0.5) * a_scale
    i_bias = 0.1 * nrm(ks[8], (N_A, H_A), f32)
    f_bias = jnp.linspace(3.0, 6.0, H_A, dtype=f32)[None, :] + 0.1 * nrm(ks[9], (N_A, H_A), f32)
    a_gate_b = jnp.concatenate([i_bias, f_bias], axis=-1)
    a_norm_w = 1.0 + 0.05 * nrm(ks[10], (N_A, VW_A), f32)
    a_w_out = nrm(ks[11], (N_A, VW_A, D_MODEL), f32) * (VW_A ** -0.5) * BETA_INIT
    b_scale = jnp.concatenate([jnp.ones((2 * QK_B,), f32), jnp.full((VW_B,), BETA_INIT, f32),
                               jnp.ones((VW_B + 2 * HV_B,), f32)])
    b_w_in = nrm(ks[12], (N_B, D_MODEL, B_IN), f32) * (D_MODEL ** -0.5) * b_scale
    b_conv_w = nrm(ks[13], (N_B, CONV_W, CONV_DIM), f32) * (CONV_W ** -0.5)
    dt = jnp.exp(jax.random.uniform(ks[14], (N_B, HV_B), f32, math.log(1e-3), math.log(1e-1)))
    b_dt_bias = dt + jnp.log(-jnp.expm1(-dt))
    b_a_log = jnp.log(jax.random.uniform(ks[15], (N_B, HV_B), f32, 1.0, 16.0))
    b_norm_w = 1.0 + 0.05 * nrm(ks[16], (N_B, VW_B), f32)
    b_w_out = nrm(ks[17], (N_B, VW_B, D_MODEL), f32) * (VW_B ** -0.5) * BETA_INIT
    mlp_w1 = nrm(ks[18], (DEPTH, D_MODEL, D_FF), f32) * (D_MODEL ** -0.5) * BETA_INIT
    mlp_w2 = nrm(ks[19], (DEPTH, D_FF, D_MODEL), f32) * (D_FF ** -0.5) * BETA_INIT
    ln1_g = 1.0 + 0.05 * nrm(ks[20], (DEPTH, D_MODEL), f32)
    ln1_b = 0.02 * nrm(ks[21], (DEPTH, D_MODEL), f32)
    ln2_g = 1.0 + 0.05 * nrm(ks[22], (DEPTH, D_MODEL), f32)
    ln2_b = 0.02 * nrm(ks[23], (DEPTH, D_MODEL), f32)
    return {"x_prompt": x_prompt, "x_sample": x_sample,
            "state_mlstm_C": state_mlstm_C, "state_mlstm_n": state_mlstm_n, "state_mlstm_m": state_mlstm_m,
            "state_gdn_S": state_gdn_S, "state_gdn_conv": state_gdn_conv,
            "a_w_in": a_w_in, "a_gate_b": a_gate_b, "a_norm_w": a_norm_w, "a_w_out": a_w_out,
            "b_w_in": b_w_in, "b_conv_w": b_conv_w, "b_dt_bias": b_dt_bias, "b_a_log": b_a_log,
            "b_norm_w": b_norm_w, "b_w_out": b_w_out,
            "mlp_w1": mlp_w1, "mlp_w2": mlp_w2,
            "ln1_g": ln1_g, "ln1_b": ln1_b, "ln2_g": ln2_g, "ln2_b": ln2_b}


def reference(x_prompt, x_sample, state_mlstm_C, state_mlstm_n, state_mlstm_m, state_gdn_S, state_gdn_conv,
              a_w_in, a_gate_b, a_norm_w, a_w_out, b_w_in, b_conv_w, b_dt_bias, b_a_log, b_norm_w, b_w_out,
              mlp_w1, mlp_w2, ln1_g, ln1_b, ln2_g, ln2_b):
    weights = (a_w_in, a_gate_b, a_norm_w, a_w_out, b_w_in, b_conv_w, b_dt_bias, b_a_log, b_norm_w, b_w_out,
               mlp_w1, mlp_w2, ln1_g, ln1_b, ln2_g, ln2_b)
    Bp = x_prompt.shape[0]
    dt = x_prompt.dtype
    C0 = jnp.zeros((N_A, Bp, H_A, DK_A, DV_A), dt)
    n0 = jnp.zeros((N_A, Bp, H_A, DK_A), dt)
    m0 = jnp.zeros((N_A, Bp, H_A), dt)
    S0 = jnp.zeros((N_B, Bp, HV_B, DK_B, DV_B), dt)
    conv0 = jnp.zeros((N_B, Bp, CONV_W - 1, CONV_DIM), dt)
    y_prompt, p_C, p_n, p_m, p_S, p_conv = trunk(x_prompt, C0, n0, m0, S0, conv0, *weights)
    y_sample, s_C, s_n, s_m, s_S, s_conv = trunk(x_sample, state_mlstm_C, state_mlstm_n, state_mlstm_m,
                                                 state_gdn_S, state_gdn_conv, *weights)
    return (y_prompt, y_sample, p_C, p_n, p_m, p_S, p_conv, s_C, s_n, s_m, s_S, s_conv)
```

```python
import os
import numpy as np
from contextlib import ExitStack
import concourse.bass as bass
import concourse.mybir as mybir
from concourse.bass_utils import run_bass_kernel_spmd

F32 = mybir.dt.float32
BF16 = mybir.dt.bfloat16
AF = mybir.ActivationFunctionType
ALU = mybir.AluOpType

NCORES = 8
D = 1024
NB = 4
TP = 512
NSQ = 4
TS = 16
NT = TP + TS
TILES = [(0, 128), (128, 128), (256, 128), (384, 128), (512, 16)]
GROUPS = [(0, 512), (512, 16)]
ALPHA = (2.0 * 4) ** 0.25
NEG = -30000.0
SLOT = 4096
NSLOT = 3
DBG = float(os.environ.get('KDBG', '99'))
KVAR = int(os.environ.get('KVAR', '0'))
USE_TR = int(os.environ.get('KTR', '1'))


def unit_plan():
    units = []
    for layer in range(4):
        j = layer // 2
        if layer % 2 == 0:
            units.append((("aG", j), 8, 16))
            for p in range(4):
                units.append((("aA", j, p), 8, 512))
                units.append((("aB", j, p), 8, 256))
            for u in range(2):
                units.append((("aO", j, u), 8, 512))
        else:
            units.append((("bG", j), 8, 32))
            for g in range(8):
                units.append((("bQ", j, g), 8, 512))
                units.append((("bZ", j, g), 8, 256))
            for u in range(4):
                units.append((("bO", j, u), 16, 256))
        for u in range(8):
            units.append((("w1", layer, u), 8, 512))
        for u in range(8):
            units.append((("w2", layer, u), 32, 128))
    offs = []
    o = 0
    for (_, kc, n) in units:
        offs.append(o)
        o += kc * n
    return units, offs, o


def _unit_data(key, inp):
    r = np.arange
    kind = key[0]
    if kind == "aG":
        W = inp["a_w_in"][key[1]]
        cols = r(3072, 3088)
    elif kind == "aA":
        W = inp["a_w_in"][key[1]]
        p = key[2]
        cols = np.concatenate([r(p * 128, p * 128 + 128), 512 + r(p * 128, p * 128 + 128),
                               1024 + r(p * 256, p * 256 + 256)])
    elif kind == "aB":
        W = inp["a_w_in"][key[1]]
        p = key[2]
        cols = 2048 + r(p * 256, p * 256 + 256)
    elif kind == "aO":
        W = inp["a_w_out"][key[1]]
        cols = r(key[2] * 512, key[2] * 512 + 512)
    elif kind == "bG":
        W = inp["b_w_in"][key[1]]
        cols = r(6144, 6176)
    elif kind == "bQ":
        W = inp["b_w_in"][key[1]]
        g = key[2]
        cols = np.concatenate([r(g * 128, g * 128 + 128), 1024 + r(g * 128, g * 128 + 128),
                               2048 + r(g * 256, g * 256 + 256)])
    elif kind == "bZ":
        W = inp["b_w_in"][key[1]]
        g = key[2]
        cols = 4096 + r(g * 256, g * 256 + 256)
    elif kind == "bO":
        W = inp["b_w_out"][key[1]]
        cols = r(key[2] * 256, key[2] * 256 + 256)
    elif kind == "w1":
        W = inp["mlp_w1"][key[1]]
        cols = r(key[2] * 512, key[2] * 512 + 512)
    elif kind == "w2":
        W = inp["mlp_w2"][key[1]]
        cols = r(key[2] * 128, key[2] * 128 + 128)
    sub = W[:, cols]
    K = sub.shape[0]
    return sub.reshape(K // 128, 128, -1).transpose(1, 0, 2).reshape(128, -1)


C_ID, C_ONE, C_MI, C_MS, C_MP, C_SMI, C_SMS, C_SMP = [i * 128 for i in range(8)]
C_SEL = 1024
C_RST = C_SEL + 2048
C_BMF = C_RST + NT
C_BMT = C_BMF + 64
NCST = C_BMT + 4

P_LN = 0
P_ANW = P_LN + 128
P_BNW = P_ANW + 16
P_CW = P_BNW + 32
P_G = P_CW + 256
NPRM = P_G + 8


def build_consts():
    c = np.zeros((128, NCST), np.float32)
    i = np.arange(128)
    row, col = i[:, None], i[None, :]
    c[:, C_ID:C_ID + 128] = np.eye(128)
    c[:, C_ONE:C_ONE + 128] = 1.0
    c[:, C_MI:C_MI + 128] = np.where(row <= col, 0.0, NEG)
    c[:, C_MS:C_MS + 128] = np.where(row < col, 0.0, NEG)
    c[:, C_MP:C_MP + 128] = np.where(col < row, 0.0, -NEG)
    same = (row // 4) == (col // 4)
    c[:, C_SMI:C_SMI + 128] = np.where((row <= col) & same, 0.0, NEG)
    c[:, C_SMS:C_SMS + 128] = np.where((row < col) & same, 0.0, NEG)
    c[:, C_SMP:C_SMP + 128] = np.where((col < row) & same, 0.0, -NEG)
    for h in range(16):
        c[h, C_SEL + h * 128:C_SEL + (h + 1) * 128] = 1.0
    rst = np.ones(NT, np.float32)
    rst[np.arange(0, TP, 128)] = 0.0
    rst[TP + np.arange(0, TS, 4)] = 0.0
    c[:, C_RST:C_RST + NT] = rst[None, :]
    bmf = np.zeros((4, 16), np.float32)
    for s in range(4):
        bmf[s, 4 * s:4 * s + 4] = 1.0
    c[:, C_BMF:C_BMF + 64] = bmf.reshape(1, 64)
    for t in range(16):
        c[t, C_BMT + t // 4] = 1.0
    return c


class Buf:
    __slots__ = ("w", "r", "const", "sem", "cnt", "name", "psum")

    def __init__(self, name="", const=False, psum=False):
        self.psum = psum
        self.w = None
        self.r = []
        self.const = const
        self.sem = None
        self.cnt = 0
        self.name = name


class TV:
    __slots__ = ("ap", "buf")

    def __init__(self, ap, buf):
        self.ap = ap
        self.buf = buf

    def __getitem__(self, k):
        return TV(self.ap[k], self.buf)

    def re(self, pat, **kw):
        return TV(self.ap.rearrange(pat, **kw), self.buf)

    def un(self, ax):
        return TV(self.ap.unsqueeze(ax), self.buf)

    def bc(self, shape):
        return TV(self.ap.to_broadcast(list(shape)), self.buf)

    def cast(self, dt):
        return TV(self.ap.bitcast(dt), self.buf)


def _ap(x):
    return x.ap if isinstance(x, TV) else x


class Prog:
    ENG = ("pe", "act", "dve", "pool", "sp")

    def __init__(self, nc, es):
        self.nc = nc
        self.es = es
        self.q = {e: [] for e in self.ENG}
        self.fence_deps = {e: [] for e in self.ENG}
        self.dma_bufs = []
        self.nsem = 0

    def op(self, eng, fn, reads=(), writes=(), dmabuf=None):
        deps = list(self.fence_deps[eng])
        self.fence_deps[eng] = []
        for b in reads:
            if b is not None and b.w is not None:
                deps.append(b.w)
            if b is not None and b.psum:
                deps.extend(h for h in b.r if h[0] != eng)
        for b in writes:
            if b is None:
                continue
            if b.w is not None:
                deps.append(b.w)
            deps.extend(b.r)
        idx = len(self.q[eng])
        if dmabuf is not None:
            if dmabuf.sem is None:
                dmabuf.sem = self.es.enter_context(self.nc.semaphore("dq%d" % self.nsem))
                self.nsem += 1
                self.dma_bufs.append(dmabuf)
            dmabuf.cnt += 16
            h = ("dma", dmabuf, dmabuf.cnt)
        else:
            h = (eng, idx)
        self.q[eng].append({"fn": fn, "deps": deps, "sig": False, "dma": dmabuf, "h": h})
        for b in reads:
            if b is not None and not b.const:
                b.r.append(h)
        for b in writes:
            if b is not None:
                b.w = h
                b.r = []
        return h

    def fence(self):
        last = []
        for e in self.ENG:
            if self.q[e]:
                h = self.q[e][-1]["h"]
                if h[0] != "dma":
                    last.append(h)
        for e in self.ENG:
            self.fence_deps[e] = list(last)

    @staticmethod
    def _bufs(*xs):
        return [x.buf for x in xs if isinstance(x, TV)]

    def mm(self, out, lhsT, rhs, start=True, stop=True):
        o, a, b = out.ap, lhsT.ap, rhs.ap
        return self.op("pe", lambda e: e.matmul(o, a, b, start=start, stop=stop),
                       self._bufs(lhsT, rhs), self._bufs(out))

    def tr(self, out, in_, ident):
        o, a, b = out.ap, in_.ap, ident.ap
        if USE_TR:
            return self.op("pe", lambda e: e.transpose(o, a, b), self._bufs(in_, ident), self._bufs(out))
        return self.op("pe", lambda e: e.matmul(o, a, b, start=True, stop=True), self._bufs(in_, ident), self._bufs(out))

    def act(self, out, in_, func, bias=None, scale=None, accum=None):
        kw = {}
        if bias is not None:
            kw["bias"] = _ap(bias)
        if scale is not None:
            kw["scale"] = _ap(scale)
        if accum is not None:
            kw["accum_out"] = accum.ap
        o, a = out.ap, in_.ap
        return self.op("act", lambda e: e.activation(o, a, func, **kw),
                       self._bufs(in_, bias, scale), self._bufs(out, accum))

    def tt(self, out, a, b, op, eng="dve"):
        o, x, y = out.ap, a.ap, b.ap
        return self.op(eng, lambda e: e.tensor_tensor(o, x, y, op), self._bufs(a, b), self._bufs(out))

    def ts(self, out, a, s1, op0, s2=None, op1=None, eng="dve", accum=None):
        o, x = out.ap, a.ap
        v1, v2 = _ap(s1), _ap(s2)
        kw = {}
        if op1 is not None:
            kw["op1"] = op1
        if accum is not None:
            kw["accum_out"] = accum.ap
        return self.op(eng, lambda e: e.tensor_scalar(o, x, v1, v2, op0, **kw),
                       self._bufs(a, s1, s2), self._bufs(out, accum))

    def stt(self, out, in0, scalar, in1, op0, op1):
        o, x, s, y = out.ap, in0.ap, _ap(scalar), in1.ap
        return self.op("dve", lambda e: e.scalar_tensor_tensor(o, x, s, y, op0, op1),
                       self._bufs(in0, scalar, in1), self._bufs(out))

    def copy(self, out, in_, eng="dve"):
        o, a = out.ap, in_.ap
        if eng == "act":
            return self.op("act", lambda e: e.copy(o, a), self._bufs(in_), self._bufs(out))
        return self.op(eng, lambda e: e.tensor_copy(o, a), self._bufs(in_), self._bufs(out))

    def memset(self, out, val, eng="dve"):
        o = out.ap
        return self.op(eng, lambda e: e.memset(o, val), [], self._bufs(out))

    def recip(self, out, in_):
        o, a = out.ap, in_.ap
        return self.op("dve", lambda e: e.reciprocal(o, a), self._bufs(in_), self._bufs(out))

    def scan(self, out, d0, d1, init, op0, op1):
        o, a, b, i = out.ap, d0.ap, d1.ap, _ap(init)
        return self.op("dve", lambda e: e.tensor_tensor_scan(o, a, b, i, op0, op1),
                       self._bufs(d0, d1, init), self._bufs(out))

    def dma(self, queue, out, in_, **kw):
        o, a = _ap(out), _ap(in_)
        sb = out if (isinstance(out, TV) and out.buf is not None) else in_
        is_load = sb is out
        return self.op(queue, lambda e: e.dma_start(out=o, in_=a, **kw),
                       [] if is_load else [sb.buf], [sb.buf] if is_load else [], dmabuf=sb.buf)

    def emit(self, block_ctx):
        nc = self.nc
        sems = {}
        for e in ("pe", "act", "dve", "pool"):
            sems[e] = self.es.enter_context(nc.semaphore("s_" + e))
        finals = [(b.sem, b.cnt) for b in self.dma_bufs]
        plans = {}
        for e in self.ENG:
            waited = {}
            plan = []
            for ins in self.q[e]:
                need = {}
                for d in ins["deps"]:
                    if d[0] == "dma":
                        key = ("dma", id(d[1]))
                        val = d[2]
                        ref = d[1]
                    else:
                        if d[0] == "pe" and e == "pe":
                            continue
                        key = d[0]
                        val = d[1]
                        ref = None
                    if waited.get(key, -1) >= val:
                        continue
                    if key not in need or need[key][0] < val:
                        need[key] = (val, ref)
                for key, (val, ref) in need.items():
                    waited[key] = val
                    if ref is None:
                        self.q[key][val]["sig"] = True
                plan.append(need)
            plans[e] = plan
        sigcount = {}
        for e in ("pe", "act", "dve", "pool"):
            c = 0
            arr = []
            for ins in self.q[e]:
                if ins["sig"] and ins["dma"] is None:
                    c += 1
                arr.append(c)
            sigcount[e] = arr

        def run(e, eng):
            for ins, need in zip(self.q[e], plans[e]):
                for key, (val, ref) in need.items():
                    if ref is not None:
                        eng.wait_ge(ref.sem, val)
                    else:
                        eng.wait_ge(sems[key], sigcount[key][val])
                r = ins["fn"](eng)
                if ins["dma"] is not None:
                    r.then_inc(ins["dma"].sem, 16)
                elif ins["sig"]:
                    r.then_inc(sems[e], 1)
            if e == "sp":
                for (s, c) in finals:
                    eng.wait_ge(s, c)

        with nc.Block() as block:
            @block.tensor
            def _(t):
                run("pe", t)

            @block.scalar
            def _(a):
                run("act", a)

            @block.vector
            def _(v):
                run("dve", v)

            @block.gpsimd
            def _(g):
                run("pool", g)

            @block.sync
            def _(s):
                run("sp", s)


def build(nlayers=4, nblocks=NB):
    nc = bass.Bass("TRN2", target_bir_lowering=False)
    units, offs, wtot = unit_plan()
    uidx = {u[0]: i for i, u in enumerate(units)}

    def din(name, shape):
        return nc.dram_tensor(name, shape, F32, kind="ExternalInput").ap()

    def dout(name, shape):
        return nc.dram_tensor(name, shape, F32, kind="ExternalOutput").ap()

    xin = din("xin", [128, 8, NB * NT])
    wts = din("wts", [128, wtot])
    cst_d = din("cst", [128, NCST])
    prm_d = din("prm", [128, NPRM])
    cin = din("cin", [2, NB, 4, 128, 4 * 129])
    min_d = din("min", [2, NB, 8, 4])
    sin = din("sin", [2, NB, 16, 128, 4 * 128])
    cvin = din("cvin", [2, NB, 32, 128, 12])
    yout = dout("yout", [128, 8, NB * NT])
    cout = dout("cout", [2, NB, 4, 128, 4 * 129])
    mout = dout("mout", [2, NB, 8, 4])
    sout = dout("sout", [2, NB, 16, 128, 4 * 128])
    cvout = dout("cvout", [2, NB, 32, 128, 12])
    pC = dout("pC", [2, 4, 128, 129])
    pm = dout("pm", [2, 8, 1])
    pS = dout("pS", [2, 128, 16 * 128])
    pcv = dout("pcv", [2, 128, 32 * 3])

    es = ExitStack()
    P = Prog(nc, es)
    cnt = [0]

    def sb(shape, dt=F32, name=None, const=False):
        cnt[0] += 1
        nm = "%s%d" % (name or "t", cnt[0])
        t = es.enter_context(nc.sbuf_tensor(nm, list(shape), dt))
        sl = tuple(slice(None) for _ in shape)
        return TV(t[sl], Buf(nm, const))

    X = sb([128, 8, NT], F32, "X")
    XB = sb([128, 8, NT], BF16, "XB")
    OT = sb([128, 16, NT], BF16, "OT")
    CST = sb([128, NCST], F32, "CST", const=True)
    PRM = sb([128, NPRM], F32, "PRM", const=True)
    WS = [sb([128, SLOT], BF16, "WS") for _ in range(NSLOT)]
    CST_ = [sb([128, 4, 129], F32, "Cst") for _ in range(2)]
    CB = sb([128, 4, 130], BF16, "Cb")
    SST = [sb([128, 16, 128], F32, "Sst") for _ in range(2)]
    SBF = sb([128, 16, 128], BF16, "Sbf")
    CARRY = [sb([128, 32, 3], F32, "carry") for _ in range(2)]
    MCAR = [sb([8, 1], F32, "mcar") for _ in range(2)]
    AEXP = sb([16, 2], F32, "aexp")
    GT = [sb([128, NT], F32, "g") for _ in range(10)]
    TM = sb([128, 5, 64], F32, "tm")
    TMX = sb([128, 5, 64], F32, "tmx")
    MPV = sb([16, 8], F32, "mpv")
    DEC = sb([16, 8], F32, "dec")
    DECB = sb([128, 16, 8], F32, "decb")
    M0T = sb([8, 4], F32, "m0t")
    MNEW = sb([8, 4], F32, "mnew")
    C0 = sb([128, 4, 129], F32, "C0")
    CNEW = sb([128, 4, 129], F32, "Cnew")
    S0 = [sb([128, 4, 128], F32, "S0") for _ in range(2)]
    SNEW = [sb([128, 4, 128], F32, "Snew") for _ in range(2)]
    XS = [sb([128, 4, 7], F32, "xs") for _ in range(2)]
    SCRW = 16512
    SCR = sb([128, SCRW], F32, "SCR")

    PSS = []
    for i in range(8):
        t = es.enter_context(nc.psum_tensor("ps%d" % i, [128, 512], F32))
        PSS.append(TV(t[:, :], Buf("ps%d" % i, psum=True)))
    rr = {"ps": 0}

    def psum():
        rr["ps"] += 1
        return PSS[rr["ps"] % 8]

    pbig = psum

    class Carver:
        def __init__(self):
            self.off = 0

        def tile(self, shape, dt=F32, name="c"):
            n = int(np.prod(shape[1:]))
            w = n if dt == F32 else (n + 1) // 2
            assert self.off + w <= SCRW, ("scratch overflow", self.off, w)
            ap = SCR.ap[:, self.off:self.off + w]
            self.off += w
            if dt != F32:
                ap = ap.bitcast(dt)[:, 0:n]
            if len(shape) == 3:
                ap = ap.rearrange("p (a b) -> p a b", a=shape[1])
            elif len(shape) == 4:
                ap = ap.rearrange("p (a b c) -> p a b c", a=shape[1], b=shape[2])
            if shape[0] != 128:
                ap = ap[0:shape[0]]
            return TV(ap, Buf(name))

        def ring(self, n, shape, dt=F32, name="r"):
            ts_ = [self.tile(shape, dt, name) for _ in range(n)]
            st = [0]

            def nxt():
                st[0] += 1
                return ts_[st[0] % n]
            return nxt

    ident = CST[:, C_ID:C_ID + 128]
    ones = CST[:, C_ONE:C_ONE + 128]
    MI, MS, MPo = (CST[:, C_MI:C_MI + 128], CST[:, C_MS:C_MS + 128], CST[:, C_MP:C_MP + 128])
    SMI, SMS, SMP = (CST[:, C_SMI:C_SMI + 128], CST[:, C_SMS:C_SMS + 128], CST[:, C_SMP:C_SMP + 128])
    SEL = CST[:, C_SEL:C_SEL + 2048]
    RST = CST[:, C_RST:C_RST + NT]
    BMF = CST[:, C_BMF:C_BMF + 64].re("p (s t) -> p s t", s=4)
    BMT = CST[:, C_BMT:C_BMT + 4]
    LNP = PRM[:, P_LN:P_LN + 128].re("p (l k c) -> p l k c", l=4, k=4)
    ANW = PRM[:, P_ANW:P_ANW + 16].re("p (j h) -> p j h", j=2)
    BNW = PRM[:, P_BNW:P_BNW + 32].re("p (j h) -> p j h", j=2)
    CW = PRM[:, P_CW:P_CW + 256].re("p (j c k) -> p j c k", j=2, c=32)
    PG = PRM[:, P_G:P_G + 8]

    wstate = {"i": 0}

    def wnext(key):
        ui = uidx[key]
        _, kc, n = units[ui]
        sz = kc * n
        slot = WS[wstate["i"] % NSLOT]
        wstate["i"] += 1
        src = wts[:, offs[ui]:offs[ui] + sz]
        dst = slot[:, 0:sz]
        if sz % 1024 == 0 and sz > 1024:
            P.dma("pool", dst.re("p (a b) -> p a b", b=1024), src.rearrange("p (a b) -> p a b", b=1024))
        else:
            P.dma("pool", dst, src)
        return dst.re("p (k n) -> p k n", k=kc)

    P.dma("sp", CST, cst_d)
    P.dma("sp", PRM, prm_d)
    for j in range(2):
        P.memset(CST_[j], 0.0)
        P.memset(SST[j], 0.0)
        P.memset(CARRY[j], 0.0)
        P.memset(MCAR[j], 0.0)
    for g in GT:
        P.memset(g, 0.0)
    P.memset(TM, 0.0)
    P.memset(TMX, 0.0)
    P.memset(MPV, 0.0)
    P.memset(DEC, 0.0)
    P.act(AEXP[0:16, 0:2], PG[0:16, 6:8], AF.Exp)

    def gate_transposes(qtys, R):
        for ti, (t0, n) in enumerate(TILES):
            ps = psum()
            for q, gq in enumerate(qtys):
                P.tr(ps[0:n, q * 16:q * 16 + R], gq[0:R, t0:t0 + n], ident[0:R, 0:R])
            P.copy(TM[0:n, ti, :].re("p (q h) -> p q h", q=4)[:, :, 0:R],
                   ps[0:n, 0:64].re("p (q h) -> p q h", q=4)[:, :, 0:R])

    def seg3(tv, R):
        return (tv[0:R, 0:TP].re("r (c l) -> r c l", l=128), tv[0:R, TP:NT].re("r (c l) -> r c l", l=4))

    def layer_norm(layer, which, cv):
        mean = cv.tile([128, 512], F32, "mean")
        msq = cv.tile([128, 512], F32, "msq")
        rstd = cv.tile([128, 512], F32, "rstd")
        mr = cv.tile([128, 512], F32, "mr")
        sqr = cv.ring(3, [128, 512], F32, "sq")
        tmpr = cv.ring(2, [128, 512], F32, "lt")
        accs = cv.tile([128, 512], F32, "accs")
        accq = cv.tile([128, 512], F32, "accq")
        for (g0, ng) in GROUPS:
            P.tt(accs[:, 0:ng], X[:, 0, g0:g0 + ng], X[:, 1, g0:g0 + ng], ALU.add, eng="pool")
            for dc in range(2, 8):
                P.tt(accs[:, 0:ng], accs[:, 0:ng], X[:, dc, g0:g0 + ng], ALU.add, eng="pool")
            for dc in range(8):
                sq = sqr()
                P.act(sq[:, 0:ng], X[:, dc, g0:g0 + ng], AF.Square)
                if dc == 0:
                    sq0 = sq
                elif dc == 1:
                    P.tt(accq[:, 0:ng], sq0[:, 0:ng], sq[:, 0:ng], ALU.add)
                else:
                    P.tt(accq[:, 0:ng], accq[:, 0:ng], sq[:, 0:ng], ALU.add)
            psS = pbig()
            P.mm(psS[:, 0:ng], ones, accs[:, 0:ng])
            P.act(mean[:, 0:ng], psS[:, 0:ng], AF.Copy, scale=1.0 / D)
            psQ = pbig()
            P.mm(psQ[:, 0:ng], ones, accq[:, 0:ng])
            P.act(msq[:, 0:ng], mean[:, 0:ng], AF.Square)
            P.stt(rstd[:, 0:ng], psQ[:, 0:ng], 1.0 / D, msq[:, 0:ng], ALU.mult, ALU.subtract)
            P.ts(rstd[:, 0:ng], rstd[:, 0:ng], 1e-5, ALU.add)
            P.act(rstd[:, 0:ng], rstd[:, 0:ng], AF.Ln)
            P.act(rstd[:, 0:ng], rstd[:, 0:ng], AF.Exp, scale=-0.5)
            P.tt(mr[:, 0:ng], mean[:, 0:ng], rstd[:, 0:ng], ALU.mult)
            for dc in range(8):
                t = tmpr()
                P.tt(t[:, 0:ng], X[:, dc, g0:g0 + ng], rstd[:, 0:ng], ALU.mult)
                P.tt(t[:, 0:ng], t[:, 0:ng], mr[:, 0:ng], ALU.subtract)
                P.ts(X[:, dc, g0:g0 + ng], t[:, 0:ng], LNP[:, layer, 2 * which, dc:dc + 1], ALU.mult,
                     LNP[:, layer, 2 * which + 1, dc:dc + 1], ALU.add)
                P.copy(XB[:, dc, g0:g0 + ng], X[:, dc, g0:g0 + ng], eng="act")

    def out_proj(kind, j, nun, kc_n, dc_per):
        for u in range(nun):
            w = wnext((kind, j, u))
            for dd in range(dc_per):
                dc = u * dc_per + dd
                for (g0, ng) in GROUPS:
                    ps = pbig()
                    for kc in range(kc_n):
                        P.mm(ps[:, 0:ng], w[:, kc, dd * 128:(dd + 1) * 128], OT[:, kc, g0:g0 + ng],
                             start=kc == 0, stop=kc == kc_n - 1)
                    P.stt(X[:, dc, g0:g0 + ng], X[:, dc, g0:g0 + ng], ALPHA, ps[:, 0:ng], ALU.mult, ALU.add)

    def mlp(layer):
        P.fence()
        cv = Carver()
        HT = cv.tile([128, 32, NT], BF16, "HT")
        sqr = cv.ring(2, [128, 512], F32, "hsq")
        for u in range(8):
            w = wnext(("w1", layer, u))
            for fc in range(4):
                for (g0, ng) in GROUPS:
                    ps = pbig()
                    for kc in range(8):
                        P.mm(ps[:, 0:ng], w[:, kc, fc * 128:(fc + 1) * 128], XB[:, kc, g0:g0 + ng],
                             start=kc == 0, stop=kc == 7)
                    sq = sqr()
                    P.act(sq[:, 0:ng], ps[:, 0:ng], AF.Square)
                    P.stt(HT[:, u * 4 + fc, g0:g0 + ng], ps[:, 0:ng], 0.0, sq[:, 0:ng], ALU.is_gt, ALU.mult)
        for dc in range(8):
            w = wnext(("w2", layer, dc))
            for (g0, ng) in GROUPS:
                ps = pbig()
                for fc in range(32):
                    P.mm(ps[:, 0:ng], w[:, fc, 0:128], HT[:, fc, g0:g0 + ng], start=fc == 0, stop=fc == 31)
                P.stt(X[:, dc, g0:g0 + ng], X[:, dc, g0:g0 + ng], ALPHA, ps[:, 0:ng], ALU.mult, ALU.add)
        P.fence()
        layer_norm(layer, 1, Carver())

    def run_sched(slot_fns, nslots, spawn):
        active = []
        free = list(range(nslots))
        pend = list(slot_fns)
        while pend or active or spawn:
            while pend and free:
                sl = free.pop(0)
                active.append([pend.pop(0)(sl), sl])
            while spawn:
                active.append([spawn.pop(0), None])
            for item in list(active):
                try:
                    next(item[0])
                except StopIteration:
                    active.remove(item)
                    if item[1] is not None:
                        free.append(item[1])

    def mlstm(blk, j):
        P.fence()
        cv = Carver()
        gi, lf, mt, bb, aa, negM, wi, enm, wk, gtmp = GT
        Cst = CST_[j]
        R = 8
        wg = wnext(("aG", j))
        for (g0, ng) in GROUPS:
            for which, gt in ((0, gi), (1, lf)):
                ps = pbig()
                for kc in range(8):
                    P.mm(ps[0:8, 0:ng], wg[:, kc, which * 8:which * 8 + 8], XB[:, kc, g0:g0 + ng],
                         start=kc == 0, stop=kc == 7)
                P.ts(gt[0:8, g0:g0 + ng], ps[0:8, 0:ng], PG[0:8, 2 * which + j:2 * which + j + 1], ALU.add)
        P.act(gi[0:8, :], gi[0:8, :], AF.Tanh, scale=1.0 / 15)
        P.ts(gi[0:8, :], gi[0:8, :], 15.0, ALU.mult)
        P.act(lf[0:8, :], lf[0:8, :], AF.Tanh, scale=1.0 / 15)
        P.act(lf[0:8, :], lf[0:8, :], AF.Exp, scale=-15.0)
        P.ts(lf[0:8, :], lf[0:8, :], 1.0, ALU.add)
        P.act(lf[0:8, :], lf[0:8, :], AF.Ln)
        P.ts(lf[0:8, :], lf[0:8, :], -1.0, ALU.mult)
        if DBG < 1:
            return
        P.dma("sp", M0T, min_d[j, blk])
        P.copy(MPV[0:8, 0:1], MCAR[j][0:8, 0:1])
        P.scan(mt[0:8, 0:TP], lf[0:8, 0:TP], gi[0:8, 0:TP], MCAR[j][0:8, 0:1], ALU.add, ALU.max)
        if DBG < 1.2:
            return
        for s in range(4):
            c0 = TP + 4 * s
            P.scan(mt[0:8, c0:c0 + 4], lf[0:8, c0:c0 + 4], gi[0:8, c0:c0 + 4], M0T[0:8, s:s + 1], ALU.add, ALU.max)
        P.scan(bb[0:8, :], RST[0:8, :], lf[0:8, :], 0.0, ALU.mult, ALU.add)
        if DBG < 1.4:
            return
        P.tt(aa[0:8, :], gi[0:8, :], bb[0:8, :], ALU.subtract)
        P.tt(negM[0:8, :], bb[0:8, :], mt[0:8, :], ALU.subtract)
        mtp, mts = seg3(mt, R)
        P.copy(MPV[0:8, 1:4], mtp[:, 0:3, 127])
        P.copy(MPV[0:8, 4:8], M0T[0:8, 0:4])
        P.copy(MCAR[j][0:8, 0:1], mt[0:8, TP - 1:TP])
        P.copy(MNEW[0:8, 0:4], mts[:, :, 3])
        P.dma("sp", mout[j, blk], MNEW)
        if DBG < 1.5:
            return
        nMp, nMs = seg3(negM, R)
        wip, wis = seg3(wi, R)
        P.tt(wip, nMp, MPV[0:8, 0:4].un(2).bc([8, 4, 128]), ALU.add)
        P.tt(wis, nMs, MPV[0:8, 4:8].un(2).bc([8, 4, 4]), ALU.add)
        P.act(wi[0:8, :], wi[0:8, :], AF.Exp)
        P.act(enm[0:8, :], mt[0:8, :], AF.Exp, scale=-1.0)
        if DBG < 1.6:
            return
        ap_, as_ = seg3(aa, R)
        wkp, wks = seg3(wk, R)
        P.tt(wkp, ap_, nMp[:, :, 127:128].bc([8, 4, 128]), ALU.add)
        P.tt(wks, as_, nMs[:, :, 3:4].bc([8, 4, 4]), ALU.add)
        P.act(wk[0:8, :], wk[0:8, :], AF.Exp)
        P.copy(DEC[0:8, 0:4], wip[:, :, 127])
        P.copy(DEC[0:8, 4:8], wis[:, :, 3])
        if DBG < 1.7:
            return
        for h in range(8):
            ps = psum()
            P.mm(ps[:, 0:8], SEL[0:8, h * 128:(h + 1) * 128], DEC[0:8, 0:8])
            P.copy(DECB[:, h, :], ps[:, 0:8])
        if DBG < 2:
            return
        gate_transposes([aa, wi, enm, wk], 8)
        if DBG < 3:
            return
        TMv = TM.re("p t (q h) -> p t q h", q=4)

        qT = cv.tile([128, NT], BF16, "qT")
        kT = cv.tile([128, NT], BF16, "kT")
        ktm = cv.tile([128, 5, 128], BF16, "ktm")
        vaug = cv.tile([128, 5, 2, 130], BF16, "vaug")
        og = cv.tile([128, 2, NT], BF16, "og")
        slotA = [cv.tile([128, 128], F32, "argA") for _ in range(4)]
        ST = cv.tile([128, 10, 128], BF16, "ST")
        hun_r = [cv.ring(3, [128, 129], F32, "hun") for _ in range(2)]
        tmp_r = [cv.ring(2, [128, 129], F32, "tmp") for _ in range(2)]
        hs_r = [cv.ring(2, [128, 128], F32, "hs") for _ in range(2)]
        kw_r = [cv.ring(2, [128, 64], BF16, "kw") for _ in range(2)]
        kwm = [cv.tile([16, 4, 64], BF16, "kwm") for _ in range(2)]
        qTm = cv.tile([128, 4, 16], F32, "qTm")
        sm_r = [cv.ring(5, [128, 16], F32, "sm") for _ in range(2)]
        junk = [cv.tile([128, 128], F32, "junk") for _ in range(2)]
        P.memset(vaug, 1.0)

        sets = [dict(qT=qT, kT=kT, ktm=ktm, vaug=vaug, og=og, qTm=qTm)]
        sets.append(dict(qT=cv.tile([128, NT], BF16, "qT2"), kT=cv.tile([128, NT], BF16, "kT2"),
                         ktm=cv.tile([128, 5, 128], BF16, "ktm2"), vaug=cv.tile([128, 5, 2, 130], BF16, "vaug2"),
                         og=cv.tile([128, 2, NT], BF16, "og2"), qTm=cv.tile([128, 4, 16], F32, "qTm2")))
        P.memset(sets[1]["vaug"], 1.0)

        def front(p, S, gate):
            wa = wnext(("aA", j, p))
            wb = wnext(("aB", j, p))
            for (g0, ng) in GROUPS:
                ps = pbig()
                for kc in range(8):
                    P.mm(ps[:, 0:ng], wa[:, kc, 0:128], XB[:, kc, g0:g0 + ng], start=kc == 0, stop=kc == 7)
                P.copy(S["qT"][:, g0:g0 + ng], ps[:, 0:ng], eng="act")
                ps = pbig()
                for kc in range(8):
                    P.mm(ps[:, 0:ng], wa[:, kc, 128:256], XB[:, kc, g0:g0 + ng], start=kc == 0, stop=kc == 7)
                P.act(S["kT"][:, g0:g0 + ng], ps[:, 0:ng], AF.Copy, scale=0.125)
                yield
            for ti, (t0, n) in enumerate(TILES):
                ps = pbig()
                for kc in range(8):
                    P.mm(ps[0:n, 0:384], XB[:, kc, t0:t0 + n], wa[:, kc, 128:512], start=kc == 0, stop=kc == 7)
                P.act(S["ktm"][0:n, ti, :], ps[0:n, 0:128], AF.Copy, scale=0.125)
                P.copy(S["vaug"][0:n, ti, :, 0:128], ps[0:n, 128:384].re("p (e d) -> p e d", e=2))
                yield
            for e in range(2):
                for (g0, ng) in GROUPS:
                    ps = pbig()
                    for kc in range(8):
                        P.mm(ps[:, 0:ng], wb[:, kc, e * 128:(e + 1) * 128], XB[:, kc, g0:g0 + ng],
                             start=kc == 0, stop=kc == 7)
                    P.act(S["og"][:, e, g0:g0 + ng], ps[:, 0:ng], AF.Sigmoid)
                    yield
            for e in range(2):
                P.copy(CB[64 * e:64 * e + 64, p, 0:129], Cst[64 * e:64 * e + 64, p, :], eng="act")
                P.tt(S["qTm"][64 * e:64 * e + 64, :, :], S["qT"][64 * e:64 * e + 64, TP:NT].un(1).bc([64, 4, 16]),
                     BMF[64 * e:64 * e + 64, :, :], ALU.mult)
            while not gate():
                yield
            P.dma("sp", C0.re("p s d -> p (s d)"), cin[j, blk, p])

        for p in range(4):
            S_ = sets[p % 2]
            if p == 0:
                run_sched([], 0, [front(0, S_, lambda: True)])
            qT, kT, ktm, vaug, og, qTm = S_["qT"], S_["kT"], S_["ktm"], S_["vaug"], S_["og"], S_["qTm"]
            cnt = [10]

            def tracked(gen, cnt=cnt):
                cnt[0] += 1

                def w():
                    yield from gen
                    cnt[0] -= 1
                return w()

            def intra_tracked(ti, e, sl, cnt=cnt):
                yield from intra_chain(ti, e, sl)
                cnt[0] -= 1

            done = {}
            spawn = []

            def intra_chain(ti, e, sl, p=p):
                t0, n = TILES[ti]
                sample = ti == 4
                h = 2 * p + e
                r0 = 64 * e
                idx = ti * 2 + e
                arg = slotA[sl]
                ps2 = psum()
                P.mm(ps2[0:n, 0:n], SEL[0:8, h * 128:h * 128 + n], negM[0:8, t0:t0 + n])
                P.tt(arg[0:n, 0:n], ps2[0:n, 0:n], (SMI if sample else MI)[0:n, 0:n], ALU.add)
                yield
                P.act(arg[0:n, 0:n], arg[0:n, 0:n], AF.Exp, bias=TMv[0:n, ti, 0, h:h + 1])
                yield
                ps1 = psum()
                P.mm(ps1[0:n, 0:n], kT[r0:r0 + 64, t0:t0 + n], qT[r0:r0 + 64, t0:t0 + n])
                P.tt(ST[0:n, idx, 0:n], ps1[0:n, 0:n], arg[0:n, 0:n], ALU.mult)
                done[(ti, e)] = True

            def out_chain(ti, e, hun, p=p):
                t0, n = TILES[ti]
                h = 2 * p + e
                sm = sm_r[e]()
                P.act(sm[0:n, 0:1], hun[0:n, 128:129], AF.Abs)
                yield
                P.tt(sm[0:n, 0:1], sm[0:n, 0:1], TMv[0:n, ti, 2, h:h + 1], ALU.max)
                yield
                P.recip(sm[0:n, 1:2], sm[0:n, 0:1])
                yield
                P.act(junk[e][0:n, :], hun[0:n, 0:128], AF.Square, scale=sm[0:n, 1:2], accum=sm[0:n, 2:3])
                yield
                P.ts(sm[0:n, 3:4], sm[0:n, 2:3], 1.0 / 128, ALU.mult, 1e-6, ALU.add)
                yield
                P.act(sm[0:n, 3:4], sm[0:n, 3:4], AF.Ln)
                yield
                P.act(sm[0:n, 3:4], sm[0:n, 3:4], AF.Exp, scale=-0.5)
                yield
                P.tt(sm[0:n, 4:5], sm[0:n, 3:4], sm[0:n, 1:2], ALU.mult)
                yield
                hs = hs_r[e]()
                P.act(hs[0:n, :], hun[0:n, 0:128], AF.Copy, scale=sm[0:n, 4:5])
                yield
                ps5 = psum()
                P.tr(ps5[:, 0:n], hs[0:n, :], ident[0:n, 0:n])
                P.stt(OT[:, h, t0:t0 + n], ps5[:, 0:n], ANW[:, j, h:h + 1], og[:, e, t0:t0 + n],
                      ALU.mult, ALU.mult)

            def rec_chain(e, p=p):
                h = 2 * p + e
                r0 = 64 * e
                for ti, (t0, n) in enumerate(TILES):
                    sample = ti == 4
                    idx = ti * 2 + e
                    while not done.get((ti, e)):
                        yield
                    ps3 = psum()
                    P.mm(ps3[0:n, 0:129], ST[0:n, idx, 0:n], vaug[0:n, ti, e, 0:129])
                    ps4 = psum()
                    if not sample:
                        P.mm(ps4[0:n, 0:129], qT[r0:r0 + 64, t0:t0 + n], CB[r0:r0 + 64, p, 0:129])
                    else:
                        for s_ in range(4):
                            P.mm(ps4[0:n, 0:129], qTm[r0:r0 + 64, s_, :], C0[r0:r0 + 64, s_, :],
                                 start=s_ == 0, stop=s_ == 3)
                    tmp = tmp_r[e]()
                    P.act(tmp[0:n, :], ps4[0:n, 0:129], AF.Copy, scale=TMv[0:n, ti, 1, h:h + 1])
                    hun = hun_r[e]()
                    P.tt(hun[0:n, :], ps3[0:n, 0:129], tmp[0:n, :], ALU.add)
                    spawn.append(tracked(out_chain(ti, e, hun)))
                    yield
                    kw = kw_r[e]()
                    P.ts(kw[0:n, :], ktm[0:n, ti, r0:r0 + 64], TMv[0:n, ti, 3, h:h + 1], ALU.mult)
                    yield
                    if not sample:
                        ps6 = psum()
                        P.mm(ps6[r0:r0 + 64, 0:129], kw[0:n, :], vaug[0:n, ti, e, 0:129])
                        P.stt(Cst[r0:r0 + 64, p, :], Cst[r0:r0 + 64, p, :], DECB[r0:r0 + 64, h, ti:ti + 1],
                              ps6[r0:r0 + 64, 0:129], ALU.mult, ALU.add)
                        yield
                        P.copy(CB[r0:r0 + 64, p, 0:129], Cst[r0:r0 + 64, p, :], eng="act")
                    else:
                        for s_ in range(4):
                            P.ts(kwm[e][0:16, s_, :], kw[0:16, :], BMT[0:16, s_:s_ + 1], ALU.mult)
                        yield
                        for s_ in range(4):
                            ps7 = psum()
                            P.mm(ps7[r0:r0 + 64, 0:129], kwm[e][0:16, s_, :], vaug[0:16, ti, e, 0:129])
                            P.stt(CNEW[r0:r0 + 64, s_, :], C0[r0:r0 + 64, s_, :], DECB[r0:r0 + 64, h, 4 + s_:5 + s_],
                                  ps7[r0:r0 + 64, 0:129], ALU.mult, ALU.add)
                    yield

            spawn.extend([tracked(rec_chain(0)), tracked(rec_chain(1))])
            if p < 3:
                spawn.append(front(p + 1, sets[(p + 1) % 2], lambda cnt=cnt: cnt[0] == 0))
            run_sched([(lambda sl, ti=ti, e=e: intra_tracked(ti, e, sl)) for ti in range(5) for e in range(2)], 4, spawn)
            P.dma("sp", cout[j, blk, p], CNEW.re("p s d -> p (s d)"))
        if blk == nblocks - 1:
            for p in range(4):
                P.dma("sp", pC[j, p], Cst[:, p, :])
            P.dma("sp", pm[j], MCAR[j])

    def gdn(blk, j):
        P.fence()
        cv = Carver()
        lb, be, gg, G, GL, kda, gtmp = GT[0:7]
        Sst = SST[j]
        carry = CARRY[j]
        R = 16
        wg = wnext(("bG", j))
        for (g0, ng) in GROUPS:
            for which, gt in ((0, lb), (1, gg)):
                ps = pbig()
                for kc in range(8):
                    P.mm(ps[0:16, 0:ng], wg[:, kc, which * 16:which * 16 + 16], XB[:, kc, g0:g0 + ng],
                         start=kc == 0, stop=kc == 7)
                if which == 0:
                    P.act(lb[0:16, g0:g0 + ng], ps[0:16, 0:ng], AF.Exp, scale=-1.0)
                else:
                    P.ts(gg[0:16, g0:g0 + ng], ps[0:16, 0:ng], PG[0:16, 4 + j:5 + j], ALU.add)
        P.ts(lb[0:16, :], lb[0:16, :], 1.0, ALU.add)
        P.act(lb[0:16, :], lb[0:16, :], AF.Ln)
        P.act(be[0:16, :], lb[0:16, :], AF.Exp, scale=-1.0)
        P.act(gg[0:16, :], gg[0:16, :], AF.Exp)
        P.ts(gg[0:16, :], gg[0:16, :], 1.0, ALU.add)
        P.act(gg[0:16, :], gg[0:16, :], AF.Ln)
        P.ts(gg[0:16, :], gg[0:16, :], AEXP[0:16, j:j + 1], ALU.mult, -1.0, ALU.mult)
        P.scan(G[0:16, :], RST[0:16, :], gg[0:16, :], 0.0, ALU.mult, ALU.add)
        P.tt(GL[0:16, :], G[0:16, :], lb[0:16, :], ALU.subtract)
        Gp, Gs = seg3(G, R)
        kp, ks = seg3(kda, R)
        P.tt(kp, Gp[:, :, 127:128].bc([16, 4, 128]), Gp, ALU.subtract)
        P.tt(ks, Gs[:, :, 3:4].bc([16, 4, 4]), Gs, ALU.subtract)
        P.copy(DEC[0:16, 0:4], Gp[:, :, 127])
        P.copy(DEC[0:16, 4:8], Gs[:, :, 3])
        for hq in range(4):
            ps = psum()
            for hh in range(4):
                h = hq * 4 + hh
                P.mm(ps[:, hh * 8:hh * 8 + 8], SEL[0:16, h * 128:(h + 1) * 128], DEC[0:16, 0:8])
            P.act(DECB[:, hq * 4:hq * 4 + 4, :], ps[:, 0:32].re("p (h s) -> p h s", h=4), AF.Exp)
        gate_transposes([G, GL, be, kda], 16)
        TMv = TM.re("p t (q h) -> p t q h", q=4)
        TXv = TMX.re("p t (q h) -> p t q h", q=4)
        P.ts(TXv[:, :, 0, :], TMv[:, :, 0, :], -1.0, ALU.mult)
        P.act(TXv[:, :, 1, :], TMv[:, :, 0, :], AF.Exp)
        P.act(TXv[:, :, 2, :], TMv[:, :, 1, :], AF.Exp)
        P.act(TXv[:, :, 3, :], TMv[:, :, 3, :], AF.Exp)
        for h in range(16):
            P.copy(SBF[:, h, :], Sst[:, h, :], eng="act")

        xpre_r = cv.ring(2, [128, 3 + TP], F32, "xpre")
        cc_r = cv.ring(2, [128, NT], F32, "cc")
        sqt = [cv.tile([128, NT], F32, "sq") for _ in range(2)]
        rnt = [cv.tile([128, NT], F32, "rn") for _ in range(2)]
        qT = cv.tile([128, NT], BF16, "gqT")
        kT = cv.tile([128, NT], BF16, "gkT")
        qf = cv.tile([128, TS], F32, "qf")
        kbg = cv.tile([128, 5, 2, 128], BF16, "kbg")
        kdec = cv.tile([128, 5, 2, 128], BF16, "kdec")
        vb = cv.tile([128, 5, 2, 128], BF16, "vb")
        zs = cv.tile([128, 2, NT], BF16, "zs")
        WCH = 4
        slot_t = [[cv.tile([128, 128], F32, "sl") for _ in range(7)] for _ in range(WCH)]
        AT = cv.tile([128, 10, 128], BF16, "AT")
        PBT = cv.tile([128, 10, 128], BF16, "PBT")
        WTN = cv.tile([128, 10, 128], BF16, "WTN")
        kk_r = cv.ring(6, [128, 128], F32, "kk")
        wTf = [cv.tile([128, 16], F32, "wTf") for _ in range(2)]
        wTm = [cv.tile([128, 4, 16], F32, "wTm") for _ in range(2)]
        qTm = cv.tile([128, 4, 16], F32, "qTm")
        kdm = [cv.tile([16, 4, 128], BF16, "kdm") for _ in range(2)]
        vn_r = [cv.ring(2, [128, 128], BF16, "vn") for _ in range(2)]
        tmp_r = [cv.ring(2, [128, 128], F32, "gtmp") for _ in range(2)]
        oall_r = [cv.ring(3, [128, 128], F32, "oall") for _ in range(2)]
        os_r = [cv.ring(2, [128, 128], F32, "os") for _ in range(2)]
        sm_r = [cv.ring(3, [128, 8], F32, "gsm") for _ in range(2)]
        junk1 = cv.tile([128, 128], F32, "gjunk")
        junk = [junk1, junk1]

        def conv_panel_g(w, wc0, panel, xsb, res):
            xpre = xpre_r()
            P.copy(xpre[:, 0:3], carry[:, panel, :])
            P.dma("sp", xsb[:, :, 0:3], cvin[j, blk, panel].rearrange("p (s r) -> p s r", r=3))
            for (g0, ng) in GROUPS:
                ps = pbig()
                for kc in range(8):
                    P.mm(ps[:, 0:ng], w[:, kc, wc0:wc0 + 128], XB[:, kc, g0:g0 + ng], start=kc == 0, stop=kc == 7)
                if g0 == 0:
                    P.copy(xpre[:, 3:3 + TP], ps[:, 0:TP], eng="act")
                else:
                    P.copy(xsb[:, :, 3:7], ps[:, 0:TS].re("p (s t) -> p s t", t=4), eng="act")
            yield
            cc = cc_r()
            res["cc"] = cc
            ccs = cc[:, TP:NT].re("p (s t) -> p s t", t=4)
            P.ts(cc[:, 0:TP], xpre[:, 0:TP], CW[:, j, panel, 0:1], ALU.mult)
            P.ts(ccs, xsb[:, :, 0:4], CW[:, j, panel, 0:1], ALU.mult)
            yield
            for k in range(1, 4):
                P.stt(cc[:, 0:TP], xpre[:, k:k + TP], CW[:, j, panel, k:k + 1], cc[:, 0:TP], ALU.mult, ALU.add)
                P.stt(ccs, xsb[:, :, k:k + 4], CW[:, j, panel, k:k + 1], ccs, ALU.mult, ALU.add)
                yield
            P.copy(carry[:, panel, :], xpre[:, TP:TP + 3])
            P.dma("sp", cvout[j, blk, panel].rearrange("p (s r) -> p s r", r=3), xsb[:, :, 4:7])
            P.act(cc[:, :], cc[:, :], AF.Silu)
            yield

        def conv_panel(w, wc0, panel, xsb):
            res = {}
            for _ in conv_panel_g(w, wc0, panel, xsb, res):
                pass
            return res["cc"]

        xsi = [0]

        def nxs():
            xsi[0] += 1
            return XS[xsi[0] % 2]

        nxt = {}
        for g in range(8):
            def qk_chain(pi, g, wq, gate):
                scl = 128.0 ** -0.5
                panel = g if pi == 0 else 8 + g
                res = {}
                for _ in conv_panel_g(wq, pi * 128, panel, nxs(), res):
                    yield
                cc = res["cc"]
                sq_, rn_ = sqt[pi], rnt[pi]
                P.act(sq_[:, :], cc[:, :], AF.Square)
                yield
                for (g0, ng) in GROUPS:
                    ps = pbig()
                    P.mm(ps[:, 0:ng], ones, sq_[:, g0:g0 + ng])
                    P.ts(rn_[:, g0:g0 + ng], ps[:, 0:ng], 1e-6, ALU.add)
                yield
                P.act(rn_[:, :], rn_[:, :], AF.Ln)
                yield
                P.act(rn_[:, :], rn_[:, :], AF.Exp, scale=-0.5)
                yield
                if pi == 0:
                    while not gate():
                        yield
                    P.stt(qT[:, :], cc[:, :], scl, rn_[:, :], ALU.mult, ALU.mult)
                    P.stt(qf[:, :], cc[:, TP:NT], scl, rn_[:, TP:NT], ALU.mult, ALU.mult)
                else:
                    P.tt(cc[:, :], cc[:, :], rn_[:, :], ALU.mult)
                    yield
                    while not gate():
                        yield
                    P.copy(kT[:, :], cc[:, :], eng="act")
                    for ti, (t0, n) in enumerate(TILES):
                        ps = psum()
                        P.tr(ps[0:n, 0:128], cc[:, t0:t0 + n], ident)
                        for e in range(2):
                            h = 2 * g + e
                            P.act(kbg[0:n, ti, e, :], ps[0:n, 0:128], AF.Copy, scale=TXv[0:n, ti, 2, h:h + 1])
                            P.ts(kdec[0:n, ti, e, :], ps[0:n, 0:128], TXv[0:n, ti, 3, h:h + 1], ALU.mult)
                        yield

            if g == 0:
                wq = wnext(("bQ", j, 0))
                run_sched([], 0, [qk_chain(0, 0, wq, lambda: True), qk_chain(1, 0, wq, lambda: True)])
            else:
                wq = nxt["wq"]
            wz = wnext(("bZ", j, g))
            cnt = [10]

            def tracked(gen, cnt=cnt):
                cnt[0] += 1

                def w():
                    yield from gen
                    cnt[0] -= 1
                return w()

            def solve_tracked(ti, e, sl, cnt=cnt):
                yield from solve_chain(ti, e, sl)
                cnt[0] -= 1
            vdone = {}
            zdone = {}

            def v_chain(e, g=g, wq=wq):
                h = 2 * g + e
                res = {}
                for _ in conv_panel_g(wq, 256 + e * 128, 16 + h, nxs(), res):
                    yield
                cc = res["cc"]
                for ti, (t0, n) in enumerate(TILES):
                    ps = psum()
                    P.tr(ps[0:n, 0:128], cc[:, t0:t0 + n], ident)
                    P.ts(vb[0:n, ti, e, :], ps[0:n, 0:128], TMv[0:n, ti, 2, h:h + 1], ALU.mult)
                    yield
                vdone[e] = True

            def z_chain(e, wz=wz):
                for (g0, ng) in GROUPS:
                    ps = pbig()
                    for kc in range(8):
                        P.mm(ps[:, 0:ng], wz[:, kc, e * 128:(e + 1) * 128], XB[:, kc, g0:g0 + ng],
                             start=kc == 0, stop=kc == 7)
                    P.act(zs[:, e, g0:g0 + ng], ps[:, 0:ng], AF.Silu)
                    yield
                zdone[e] = True

            P.tt(qTm[:, :, :], qf[:, :].un(1).bc([128, 4, 16]), BMF, ALU.mult)
            S0e, SNe = [], []
            for e in range(2):
                h = 2 * g + e
                P.dma("sp", S0[e].re("p s d -> p (s d)"), sin[j, blk, h])
                S0e.append(S0[e])
                SNe.append(SNEW[e])

            kkt = {}

            def solve_chain(ti, e, sl):
                t0, n = TILES[ti]
                sample = ti == 4
                nlev = 2 if sample else 7
                h = 2 * g + e
                idx = ti * 2 + e
                tB, tM, tD, tP, u0, u1, u2 = slot_t[sl]
                if e == 0:
                    psKKp = psum()
                    P.mm(psKKp[0:n, 0:n], kT[:, t0:t0 + n], kT[:, t0:t0 + n])
                    KKs = kk_r()
                    P.copy(KKs[0:n, 0:n], psKKp[0:n, 0:n], eng="act")
                    psQKp = psum()
                    P.mm(psQKp[0:n, 0:n], kT[:, t0:t0 + n], qT[:, t0:t0 + n])
                    QKs = kk_r()
                    P.copy(QKs[0:n, 0:n], psQKp[0:n, 0:n], eng="act")
                    kkt[ti] = (KKs, QKs)
                KKs, QKs = kkt[ti]
                psG = psum()
                P.mm(psG[0:n, 0:n], SEL[0:16, h * 128:h * 128 + n], G[0:16, t0:t0 + n])
                P.tt(tM[0:n, 0:n], psG[0:n, 0:n], (SMP if sample else MPo)[0:n, 0:n], ALU.add)
                P.tt(tD[0:n, 0:n], psG[0:n, 0:n], (SMI if sample else MI)[0:n, 0:n], ALU.add)
                psGL = psum()
                P.mm(psGL[0:n, 0:n], SEL[0:16, h * 128:h * 128 + n], GL[0:16, t0:t0 + n])
                P.tt(tB[0:n, 0:n], psGL[0:n, 0:n], (SMS if sample else MS)[0:n, 0:n], ALU.add)
                yield
                P.act(tB[0:n, 0:n], tB[0:n, 0:n], AF.Exp, bias=TXv[0:n, ti, 0, h:h + 1])
                P.act(tM[0:n, 0:n], tM[0:n, 0:n], AF.Exp, bias=TMv[0:n, ti, 1, h:h + 1], scale=-1.0)
                P.act(tD[0:n, 0:n], tD[0:n, 0:n], AF.Exp, bias=TXv[0:n, ti, 0, h:h + 1])
                yield
                P.tt(tB[0:n, 0:n], KKs[0:n, 0:n], tB[0:n, 0:n], ALU.mult, eng="pool")
                P.tt(tM[0:n, 0:n], KKs[0:n, 0:n], tM[0:n, 0:n], ALU.mult, eng="pool")
                P.tt(AT[0:n, idx, 0:n], QKs[0:n, 0:n], tD[0:n, 0:n], ALU.mult, eng="pool")
                P.tt(tP[0:n, 0:n], ident[0:n, 0:n], tB[0:n, 0:n], ALU.subtract, eng="pool")
                yield
                Nc, Mc, Pc = tB, tM, tP
                spare = [u0, u1, u2]
                for lev in range(1, nlev):
                    M2, N2, Pn = spare
                    psM = psum()
                    P.mm(psM[0:n, 0:n], Nc[0:n, 0:n], Mc[0:n, 0:n])
                    P.copy(M2[0:n, 0:n], psM[0:n, 0:n], eng="act")
                    yield
                    psP = psum()
                    P.mm(psP[0:n, 0:n], M2[0:n, 0:n], Pc[0:n, 0:n])
                    P.tt(Pn[0:n, 0:n], psP[0:n, 0:n], Pc[0:n, 0:n], ALU.add)
                    if lev < nlev - 1:
                        psN = psum()
                        P.tr(psN[0:n, 0:n], M2[0:n, 0:n], ident[0:n, 0:n])
                        P.copy(N2[0:n, 0:n], psN[0:n, 0:n], eng="dve")
                    yield
                    spare = [Mc, Nc, Pc]
                    Pc, Mc = Pn, M2
                    if lev < nlev - 1:
                        Nc = N2
                    else:
                        spare[1] = N2
                P.copy(PBT[0:n, idx, 0:n], Pc[0:n, 0:n], eng="act")
                yield
                psw = psum()
                P.mm(psw[:, 0:n], kbg[0:n, ti, e, :], PBT[0:n, idx, 0:n])
                if not sample:
                    P.act(WTN[:, idx, 0:n], psw[:, 0:n], AF.Copy, scale=-1.0)
                else:
                    P.act(wTf[e][:, 0:n], psw[:, 0:n], AF.Copy, scale=-1.0)
                    yield
                    P.tt(wTm[e][:, :, :], wTf[e][:, :].un(1).bc([128, 4, 16]), BMF, ALU.mult)
                done[(ti, e)] = True

            done = {}
            spawn = []

            def out_chain(ti, e, oall):
                t0, n = TILES[ti]
                h = 2 * g + e
                sm = sm_r[e]()
                P.act(junk[e][0:n, :], oall[0:n, :], AF.Square, accum=sm[0:n, 0:1])
                yield
                while not zdone.get(e):
                    yield
                P.ts(sm[0:n, 1:2], sm[0:n, 0:1], 1.0 / 128, ALU.mult, 1e-6, ALU.add)
                yield
                P.act(sm[0:n, 1:2], sm[0:n, 1:2], AF.Ln)
                yield
                P.act(sm[0:n, 1:2], sm[0:n, 1:2], AF.Exp, scale=-0.5)
                yield
                osb = os_r[e]()
                P.act(osb[0:n, :], oall[0:n, :], AF.Copy, scale=sm[0:n, 1:2])
                yield
                ps5 = psum()
                P.tr(ps5[:, 0:n], osb[0:n, :], ident[0:n, 0:n])
                P.stt(OT[:, h, t0:t0 + n], ps5[:, 0:n], BNW[:, j, h:h + 1], zs[:, e, t0:t0 + n],
                      ALU.mult, ALU.mult)

            def rec_chain(e):
                h = 2 * g + e
                for ti, (t0, n) in enumerate(TILES):
                    sample = ti == 4
                    idx = ti * 2 + e
                    while not (done.get((ti, e)) and vdone.get(e)):
                        yield
                    psv = psum()
                    P.mm(psv[0:n, 0:128], PBT[0:n, idx, 0:n], vb[0:n, ti, e, :], start=True, stop=False)
                    if not sample:
                        P.mm(psv[0:n, 0:128], WTN[:, idx, 0:n], SBF[:, h, :], start=False, stop=True)
                    else:
                        for s_ in range(4):
                            P.mm(psv[0:n, 0:128], wTm[e][:, s_, :], S0e[e][:, s_, :], start=False, stop=s_ == 3)
                    vn = vn_r[e]()
                    P.copy(vn[0:n, :], psv[0:n, 0:128], eng="act")
                    yield
                    pso1 = psum()
                    if not sample:
                        P.mm(pso1[0:n, 0:128], qT[:, t0:t0 + n], SBF[:, h, :])
                    else:
                        for s_ in range(4):
                            P.mm(pso1[0:n, 0:128], qTm[:, s_, :], S0e[e][:, s_, :], start=s_ == 0, stop=s_ == 3)
                    tmp = tmp_r[e]()
                    P.act(tmp[0:n, :], pso1[0:n, 0:128], AF.Copy, scale=TXv[0:n, ti, 1, h:h + 1])
                    pso2 = psum()
                    P.mm(pso2[0:n, 0:128], AT[0:n, idx, 0:n], vn[0:n, :])
                    oall = oall_r[e]()
                    P.tt(oall[0:n, :], pso2[0:n, 0:128], tmp[0:n, :], ALU.add)
                    if not sample:
                        psS = psum()
                        P.mm(psS[:, 0:128], kdec[0:n, ti, e, :], vn[0:n, :])
                        P.stt(Sst[:, h, :], Sst[:, h, :], DECB[:, h, ti:ti + 1], psS[:, 0:128], ALU.mult, ALU.add)
                        P.copy(SBF[:, h, :], Sst[:, h, :], eng="act")
                    else:
                        for s_ in range(4):
                            P.ts(kdm[e][0:16, s_, :], kdec[0:16, ti, e, :], BMT[0:16, s_:s_ + 1], ALU.mult)
                        for s_ in range(4):
                            psS = psum()
                            P.mm(psS[:, 0:128], kdm[e][0:16, s_, :], vn[0:16, :])
                            P.stt(SNe[e][:, s_, :], S0e[e][:, s_, :], DECB[:, h, 4 + s_:5 + s_], psS[:, 0:128],
                                  ALU.mult, ALU.add)
                    spawn.append(tracked(out_chain(ti, e, oall)))
                    yield

            def next_front(g=g, vdone=None, cnt=cnt):
                while not (vdone_ref.get(0) and vdone_ref.get(1)):
                    yield
                wqn = wnext(("bQ", j, g + 1))
                nxt["wq"] = wqn
                gate = lambda: cnt[0] == 0
                alive = [qk_chain(0, g + 1, wqn, gate), qk_chain(1, g + 1, wqn, gate)]
                while alive:
                    for c in list(alive):
                        try:
                            next(c)
                        except StopIteration:
                            alive.remove(c)
                    yield

            vdone_ref = vdone
            spawn.extend([tracked(v_chain(0)), tracked(v_chain(1)), tracked(rec_chain(0)), tracked(rec_chain(1)),
                          tracked(z_chain(0)), tracked(z_chain(1))])
            if g < 7:
                spawn.append(next_front())
            run_sched([(lambda sl, ti=ti, e=e: solve_tracked(ti, e, sl)) for ti in range(5) for e in range(2)], WCH, spawn)
            for e in range(2):
                h = 2 * g + e
                P.dma("sp", sout[j, blk, h], SNe[e].re("p s d -> p (s d)"))
        if blk == nblocks - 1:
            P.dma("sp", pS[j], Sst.re("p h d -> p (h d)"))
            P.dma("sp", pcv[j], carry.re("p c r -> p (c r)"))

    for blk in range(nblocks):
        P.dma("sp", X, xin[:, :, blk * NT:(blk + 1) * NT])
        for dc in range(8):
            P.copy(XB[:, dc, :], X[:, dc, :], eng="act")
        for layer in range(nlayers):
            j = layer // 2
            if layer % 2 == 0:
                mlstm(blk, j)
                if DBG < 10:
                    continue
                P.fence()
                out_proj("aO", j, 2, 8, 4)
            else:
                gdn(blk, j)
                P.fence()
                out_proj("bO", j, 4, 16, 2)
            if DBG < 11:
                continue
            P.fence()
            layer_norm(layer, 0, Carver())
            if DBG < 12:
                continue
            mlp(layer)
        P.dma("sp", yout[:, :, blk * NT:(blk + 1) * NT], X)
    P.emit(None)
    es.close()
    return nc


_CACHE = {}


def _prep_shared(inp):
    units, offs, wtot = unit_plan()
    wts = np.empty((128, wtot), np.float32)
    for (key, kc, n), o in zip(units, offs):
        wts[:, o:o + kc * n] = _unit_data(key, inp)
    prm = np.zeros((128, NPRM), np.float32)
    ln = np.stack([inp["ln1_g"], inp["ln1_b"], inp["ln2_g"], inp["ln2_b"]], axis=1)
    prm[:, P_LN:P_LN + 128] = ln.reshape(4, 4, 8, 128).transpose(3, 0, 1, 2).reshape(128, 128)
    prm[:, P_ANW:P_ANW + 16] = inp["a_norm_w"].reshape(2, 8, 128).transpose(2, 0, 1).reshape(128, 16)
    prm[:, P_BNW:P_BNW + 32] = inp["b_norm_w"].reshape(2, 16, 128).transpose(2, 0, 1).reshape(128, 32)
    prm[:, P_CW:P_CW + 256] = inp["b_conv_w"].reshape(2, 4, 32, 128).transpose(3, 0, 2, 1).reshape(128, 256)
    for j in range(2):
        prm[0:8, P_G + 0 + j] = inp["a_gate_b"][j, 0:8]
        prm[0:8, P_G + 2 + j] = inp["a_gate_b"][j, 8:16]
        prm[0:16, P_G + 4 + j] = inp["b_dt_bias"][j]
        prm[0:16, P_G + 6 + j] = inp["b_a_log"][j]
    return wts, prm, build_consts()


def make_in_maps(inp, cores=range(NCORES)):
    wts, prm, cst = _prep_shared(inp)
    in_maps = []
    for c in cores:
        sq = c * 16 + np.arange(16)
        xp = inp["x_prompt"][c].reshape(NB, TP, D)
        xs = inp["x_sample"][sq].reshape(NB, TS, D)
        xt = np.concatenate([xp, xs], axis=1)
        xin = xt.reshape(NB * NT, 8, 128).transpose(2, 1, 0)
        C = inp["state_mlstm_C"][:, sq]
        n_ = inp["state_mlstm_n"][:, sq]
        Ca = np.concatenate([C, n_[..., None]], axis=-1)
        Ca = Ca.reshape(2, NB, 4, 4, 2, 64, 129)
        cin = Ca.transpose(0, 1, 3, 4, 5, 2, 6).reshape(2, NB, 4, 128, 4 * 129)
        m_ = inp["state_mlstm_m"][:, sq].reshape(2, NB, 4, 8).transpose(0, 1, 3, 2)
        S = inp["state_gdn_S"][:, sq].reshape(2, NB, 4, 16, 128, 128)
        sin = S.transpose(0, 1, 3, 4, 2, 5).reshape(2, NB, 16, 128, 4 * 128)
        cvs = inp["state_gdn_conv"][:, sq].reshape(2, NB, 4, 3, 32, 128)
        cvin = cvs.transpose(0, 1, 4, 5, 2, 3).reshape(2, NB, 32, 128, 12)
        in_maps.append({
            "xin": np.ascontiguousarray(xin, np.float32), "wts": wts, "cst": cst, "prm": prm,
            "cin": np.ascontiguousarray(cin, np.float32), "min": np.ascontiguousarray(m_, np.float32),
            "sin": np.ascontiguousarray(sin, np.float32), "cvin": np.ascontiguousarray(cvin, np.float32),
        })
    return in_maps


def assemble(R, cores=range(NCORES)):
    y_prompt = np.zeros((8, 2048, D), np.float32)
    y_sample = np.zeros((128, 4, D), np.float32)
    p_C = np.zeros((2, 8, 8, 64, 128), np.float32)
    p_n = np.zeros((2, 8, 8, 64), np.float32)
    p_m = np.zeros((2, 8, 8), np.float32)
    p_S = np.zeros((2, 8, 16, 128, 128), np.float32)
    p_cv = np.zeros((2, 8, 3, 4096), np.float32)
    s_C = np.zeros((2, 128, 8, 64, 128), np.float32)
    s_n = np.zeros((2, 128, 8, 64), np.float32)
    s_m = np.zeros((2, 128, 8), np.float32)
    s_S = np.zeros((2, 128, 16, 128, 128), np.float32)
    s_cv = np.zeros((2, 128, 3, 4096), np.float32)
    for i, c in enumerate(cores):
        r = R[i]
        sq = c * 16 + np.arange(16)
        y = r["yout"].transpose(2, 1, 0).reshape(NB, NT, D)
        y_prompt[c] = y[:, :TP].reshape(2048, D)
        y_sample[sq] = y[:, TP:].reshape(16, 4, D)
        pc = r["pC"].reshape(2, 4, 2, 64, 129).reshape(2, 8, 64, 129)
        p_C[:, c] = pc[..., :128]
        p_n[:, c] = pc[..., 128]
        p_m[:, c] = r["pm"].reshape(2, 8)
        p_S[:, c] = r["pS"].reshape(2, 128, 16, 128).transpose(0, 2, 1, 3)
        p_cv[:, c] = r["pcv"].reshape(2, 128, 32, 3).transpose(0, 3, 2, 1).reshape(2, 3, 4096)
        co = r["cout"].reshape(2, NB, 4, 2, 64, 4, 129).transpose(0, 1, 5, 2, 3, 4, 6).reshape(2, 16, 8, 64, 129)
        s_C[:, sq] = co[..., :128]
        s_n[:, sq] = co[..., 128]
        s_m[:, sq] = r["mout"].transpose(0, 1, 3, 2).reshape(2, 16, 8)
        so = r["sout"].reshape(2, NB, 16, 128, 4, 128).transpose(0, 1, 4, 2, 3, 5).reshape(2, 16, 16, 128, 128)
        s_S[:, sq] = so
        cvo = r["cvout"].reshape(2, NB, 32, 128, 4, 3).transpose(0, 1, 4, 5, 2, 3).reshape(2, 16, 3, 4096)
        s_cv[:, sq] = cvo
    return (y_prompt, y_sample, p_C, p_n, p_m, p_S, p_cv, s_C, s_n, s_m, s_S, s_cv)


def kernel(**inputs):
    inp = {k: np.asarray(v) for k, v in inputs.items()}
    if "nc" not in _CACHE:
        _CACHE["nc"] = build()
    nc = _CACHE["nc"]
    in_maps = make_in_maps(inp)
    res = run_bass_kernel_spmd(nc, in_maps, core_ids=list(range(NCORES)))
    return assemble(res.results)
```

```python
import os
import numpy as np
from contextlib import ExitStack
import concourse.bass as bass
import concourse.mybir as mybir
from concourse.bass_utils import run_bass_kernel_spmd

F32 = mybir.dt.float32
BF16 = mybir.dt.bfloat16
AF = mybir.ActivationFunctionType
ALU = mybir.AluOpType

NCORES = 8
D = 1024
NB = 4
TP = 512
NSQ = 4
TS = 16
NT = TP + TS
TILES = [(0, 128), (128, 128), (256, 128), (384, 128), (512, 16)]
GROUPS = [(0, 512), (512, 16)]
ALPHA = (2.0 * 4) ** 0.25
NEG = -30000.0
SLOT = 4096
NSLOT = 3
DBG = float(os.environ.get('KDBG', '99'))
KVAR = int(os.environ.get('KVAR', '0'))
USE_TR = int(os.environ.get('KTR', '1'))


def unit_plan():
    units = []
    for layer in range(4):
        j = layer // 2
        if layer % 2 == 0:
            units.append((("aG", j), 8, 16))
            for p in range(4):
                units.append((("aA", j, p), 8, 512))
                units.append((("aB", j, p), 8, 256))
            for u in range(2):
                units.append((("aO", j, u), 8, 512))
        else:
            units.append((("bG", j), 8, 32))
            for g in range(8):
                units.append((("bQ", j, g), 8, 512))
                units.append((("bZ", j, g), 8, 256))
            for u in range(4):
                units.append((("bO", j, u), 16, 256))
        for u in range(8):
            units.append((("w1", layer, u), 8, 512))
        for u in range(8):
            units.append((("w2", layer, u), 32, 128))
    offs = []
    o = 0
    for (_, kc, n) in units:
        offs.append(o)
        o += kc * n
    return units, offs, o


def _unit_data(key, inp):
    r = np.arange
    kind = key[0]
    if kind == "aG":
        W = inp["a_w_in"][key[1]]
        cols = r(3072, 3088)
    elif kind == "aA":
        W = inp["a_w_in"][key[1]]
        p = key[2]
        cols = np.concatenate([r(p * 128, p * 128 + 128), 512 + r(p * 128, p * 128 + 128),
                               1024 + r(p * 256, p * 256 + 256)])
    elif kind == "aB":
        W = inp["a_w_in"][key[1]]
        p = key[2]
        cols = 2048 + r(p * 256, p * 256 + 256)
    elif kind == "aO":
        W = inp["a_w_out"][key[1]]
        cols = r(key[2] * 512, key[2] * 512 + 512)
    elif kind == "bG":
        W = inp["b_w_in"][key[1]]
        cols = r(6144, 6176)
    elif kind == "bQ":
        W = inp["b_w_in"][key[1]]
        g = key[2]
        cols = np.concatenate([r(g * 128, g * 128 + 128), 1024 + r(g * 128, g * 128 + 128),
                               2048 + r(g * 256, g * 256 + 256)])
    elif kind == "bZ":
        W = inp["b_w_in"][key[1]]
        g = key[2]
        cols = 4096 + r(g * 256, g * 256 + 256)
    elif kind == "bO":
        W = inp["b_w_out"][key[1]]
        cols = r(key[2] * 256, key[2] * 256 + 256)
    elif kind == "w1":
        W = inp["mlp_w1"][key[1]]
        cols = r(key[2] * 512, key[2] * 512 + 512)
    elif kind == "w2":
        W = inp["mlp_w2"][key[1]]
        cols = r(key[2] * 128, key[2] * 128 + 128)
    sub = W[:, cols]
    K = sub.shape[0]
    return sub.reshape(K // 128, 128, -1).transpose(1, 0, 2).reshape(128, -1)


C_ID, C_ONE, C_MI, C_MS, C_MP, C_SMI, C_SMS, C_SMP = [i * 128 for i in range(8)]
C_SEL = 1024
C_RST = C_SEL + 2048
C_BMF = C_RST + NT
C_BMT = C_BMF + 64
NCST = C_BMT + 4

P_LN = 0
P_ANW = P_LN + 128
P_BNW = P_ANW + 16
P_CW = P_BNW + 32
P_G = P_CW + 256
NPRM = P_G + 8


def build_consts():
    c = np.zeros((128, NCST), np.float32)
    i = np.arange(128)
    row, col = i[:, None], i[None, :]
    c[:, C_ID:C_ID + 128] = np.eye(128)
    c[:, C_ONE:C_ONE + 128] = 1.0
    c[:, C_MI:C_MI + 128] = np.where(row <= col, 0.0, NEG)
    c[:, C_MS:C_MS + 128] = np.where(row < col, 0.0, NEG)
    c[:, C_MP:C_MP + 128] = np.where(col < row, 0.0, -NEG)
    same = (row // 4) == (col // 4)
    c[:, C_SMI:C_SMI + 128] = np.where((row <= col) & same, 0.0, NEG)
    c[:, C_SMS:C_SMS + 128] = np.where((row < col) & same, 0.0, NEG)
    c[:, C_SMP:C_SMP + 128] = np.where((col < row) & same, 0.0, -NEG)
    for h in range(16):
        c[h, C_SEL + h * 128:C_SEL + (h + 1) * 128] = 1.0
    rst = np.ones(NT, np.float32)
    rst[np.arange(0, TP, 128)] = 0.0
    rst[TP + np.arange(0, TS, 4)] = 0.0
    c[:, C_RST:C_RST + NT] = rst[None, :]
    bmf = np.zeros((4, 16), np.float32)
    for s in range(4):
        bmf[s, 4 * s:4 * s + 4] = 1.0
    c[:, C_BMF:C_BMF + 64] = bmf.reshape(1, 64)
    for t in range(16):
        c[t, C_BMT + t // 4] = 1.0
    return c


class Buf:
    __slots__ = ("w", "r", "const", "sem", "cnt", "name", "psum")

    def __init__(self, name="", const=False, psum=False):
        self.psum = psum
        self.w = None
        self.r = []
        self.const = const
        self.sem = None
        self.cnt = 0
        self.name = name


class TV:
    __slots__ = ("ap", "buf")

    def __init__(self, ap, buf):
        self.ap = ap
        self.buf = buf

    def __getitem__(self, k):
        return TV(self.ap[k], self.buf)

    def re(self, pat, **kw):
        return TV(self.ap.rearrange(pat, **kw), self.buf)

    def un(self, ax):
        return TV(self.ap.unsqueeze(ax), self.buf)

    def bc(self, shape):
        return TV(self.ap.to_broadcast(list(shape)), self.buf)

    def cast(self, dt):
        return TV(self.ap.bitcast(dt), self.buf)


def _ap(x):
    return x.ap if isinstance(x, TV) else x


class Prog:
    ENG = ("pe", "act", "dve", "pool", "sp")

    def __init__(self, nc, es):
        self.nc = nc
        self.es = es
        self.q = {e: [] for e in self.ENG}
        self.fence_deps = {e: [] for e in self.ENG}
        self.dma_bufs = []
        self.nsem = 0

    def op(self, eng, fn, reads=(), writes=(), dmabuf=None):
        deps = list(self.fence_deps[eng])
        self.fence_deps[eng] = []
        for b in reads:
            if b is not None and b.w is not None:
                deps.append(b.w)
            if b is not None and b.psum:
                deps.extend(h for h in b.r if h[0] != eng)
        for b in writes:
            if b is None:
                continue
            if b.w is not None:
                deps.append(b.w)
            deps.extend(b.r)
        idx = len(self.q[eng])
        if dmabuf is not None:
            if dmabuf.sem is None:
                dmabuf.sem = self.es.enter_context(self.nc.semaphore("dq%d" % self.nsem))
                self.nsem += 1
                self.dma_bufs.append(dmabuf)
            dmabuf.cnt += 16
            h = ("dma", dmabuf, dmabuf.cnt)
        else:
            h = (eng, idx)
        self.q[eng].append({"fn": fn, "deps": deps, "sig": False, "dma": dmabuf, "h": h})
        for b in reads:
            if b is not None and not b.const:
                b.r.append(h)
        for b in writes:
            if b is not None:
                b.w = h
                b.r = []
        return h

    def fence(self):
        last = []
        for e in self.ENG:
            if self.q[e]:
                h = self.q[e][-1]["h"]
                if h[0] != "dma":
                    last.append(h)
        for e in self.ENG:
            self.fence_deps[e] = list(last)

    @staticmethod
    def _bufs(*xs):
        return [x.buf for x in xs if isinstance(x, TV)]

    def mm(self, out, lhsT, rhs, start=True, stop=True):
        o, a, b = out.ap, lhsT.ap, rhs.ap
        return self.op("pe", lambda e: e.matmul(o, a, b, start=start, stop=stop),
                       self._bufs(lhsT, rhs), self._bufs(out))

    def tr(self, out, in_, ident):
        o, a, b = out.ap, in_.ap, ident.ap
        if USE_TR:
            return self.op("pe", lambda e: e.transpose(o, a, b), self._bufs(in_, ident), self._bufs(out))
        return self.op("pe", lambda e: e.matmul(o, a, b, start=True, stop=True), self._bufs(in_, ident), self._bufs(out))

    def act(self, out, in_, func, bias=None, scale=None, accum=None):
        kw = {}
        if bias is not None:
            kw["bias"] = _ap(bias)
        if scale is not None:
            kw["scale"] = _ap(scale)
        if accum is not None:
            kw["accum_out"] = accum.ap
        o, a = out.ap, in_.ap
        return self.op("act", lambda e: e.activation(o, a, func, **kw),
                       self._bufs(in_, bias, scale), self._bufs(out, accum))

    def tt(self, out, a, b, op, eng="dve"):
        o, x, y = out.ap, a.ap, b.ap
        return self.op(eng, lambda e: e.tensor_tensor(o, x, y, op), self._bufs(a, b), self._bufs(out))

    def ts(self, out, a, s1, op0, s2=None, op1=None, eng="dve", accum=None):
        o, x = out.ap, a.ap
        v1, v2 = _ap(s1), _ap(s2)
        kw = {}
        if op1 is not None:
            kw["op1"] = op1
        if accum is not None:
            kw["accum_out"] = accum.ap
        return self.op(eng, lambda e: e.tensor_scalar(o, x, v1, v2, op0, **kw),
                       self._bufs(a, s1, s2), self._bufs(out, accum))

    def stt(self, out, in0, scalar, in1, op0, op1):
        o, x, s, y = out.ap, in0.ap, _ap(scalar), in1.ap
        return self.op("dve", lambda e: e.scalar_tensor_tensor(o, x, s, y, op0, op1),
                       self._bufs(in0, scalar, in1), self._bufs(out))

    def copy(self, out, in_, eng="dve"):
        o, a = out.ap, in_.ap
        if eng == "act":
            return self.op("act", lambda e: e.copy(o, a), self._bufs(in_), self._bufs(out))
        return self.op(eng, lambda e: e.tensor_copy(o, a), self._bufs(in_), self._bufs(out))

    def memset(self, out, val, eng="dve"):
        o = out.ap
        return self.op(eng, lambda e: e.memset(o, val), [], self._bufs(out))

    def recip(self, out, in_):
        o, a = out.ap, in_.ap
        return self.op("dve", lambda e: e.reciprocal(o, a), self._bufs(in_), self._bufs(out))

    def scan(self, out, d0, d1, init, op0, op1):
        o, a, b, i = out.ap, d0.ap, d1.ap, _ap(init)
        return self.op("dve", lambda e: e.tensor_tensor_scan(o, a, b, i, op0, op1),
                       self._bufs(d0, d1, init), self._bufs(out))

    def dma(self, queue, out, in_, **kw):
        o, a = _ap(out), _ap(in_)
        sb = out if (isinstance(out, TV) and out.buf is not None) else in_
        is_load = sb is out
        return self.op(queue, lambda e: e.dma_start(out=o, in_=a, **kw),
                       [] if is_load else [sb.buf], [sb.buf] if is_load else [], dmabuf=sb.buf)

    def emit(self, block_ctx):
        nc = self.nc
        sems = {}
        for e in ("pe", "act", "dve", "pool"):
            sems[e] = self.es.enter_context(nc.semaphore("s_" + e))
        finals = [(b.sem, b.cnt) for b in self.dma_bufs]
        plans = {}
        for e in self.ENG:
            waited = {}
            plan = []
            for ins in self.q[e]:
                need = {}
                for d in ins["deps"]:
                    if d[0] == "dma":
                        key = ("dma", id(d[1]))
                        val = d[2]
                        ref = d[1]
                    else:
                        if d[0] == "pe" and e == "pe":
                            continue
                        key = d[0]
                        val = d[1]
                        ref = None
                    if waited.get(key, -1) >= val:
                        continue
                    if key not in need or need[key][0] < val:
                        need[key] = (val, ref)
                for key, (val, ref) in need.items():
                    waited[key] = val
                    if ref is None:
                        self.q[key][val]["sig"] = True
                plan.append(need)
            plans[e] = plan
        sigcount = {}
        for e in ("pe", "act", "dve", "pool"):
            c = 0
            arr = []
            for ins in self.q[e]:
                if ins["sig"] and ins["dma"] is None:
                    c += 1
                arr.append(c)
            sigcount[e] = arr

        def run(e, eng):
            for ins, need in zip(self.q[e], plans[e]):
                for key, (val, ref) in need.items():
                    if ref is not None:
                        eng.wait_ge(ref.sem, val)
                    else:
                        eng.wait_ge(sems[key], sigcount[key][val])
                r = ins["fn"](eng)
                if ins["dma"] is not None:
                    r.then_inc(ins["dma"].sem, 16)
                elif ins["sig"]:
                    r.then_inc(sems[e], 1)
            if e == "sp":
                for (s, c) in finals:
                    eng.wait_ge(s, c)

        with nc.Block() as block:
            @block.tensor
            def _(t):
                run("pe", t)

            @block.scalar
            def _(a):
                run("act", a)

            @block.vector
            def _(v):
                run("dve", v)

            @block.gpsimd
            def _(g):
                run("pool", g)

            @block.sync
            def _(s):
                run("sp", s)


def build(nlayers=4, nblocks=NB):
    nc = bass.Bass("TRN2", target_bir_lowering=False)
    units, offs, wtot = unit_plan()
    uidx = {u[0]: i for i, u in enumerate(units)}

    def din(name, shape):
        return nc.dram_tensor(name, shape, F32, kind="ExternalInput").ap()

    def dout(name, shape):
        return nc.dram_tensor(name, shape, F32, kind="ExternalOutput").ap()

    xin = din("xin", [128, 8, NB * NT])
    wts = din("wts", [128, wtot])
    cst_d = din("cst", [128, NCST])
    prm_d = din("prm", [128, NPRM])
    cin = din("cin", [2, NB, 4, 128, 4 * 129])
    min_d = din("min", [2, NB, 8, 4])
    sin = din("sin", [2, NB, 16, 128, 4 * 128])
    cvin = din("cvin", [2, NB, 32, 128, 12])
    yout = dout("yout", [128, 8, NB * NT])
    cout = dout("cout", [2, NB, 4, 128, 4 * 129])
    mout = dout("mout", [2, NB, 8, 4])
    sout = dout("sout", [2, NB, 16, 128, 4 * 128])
    cvout = dout("cvout", [2, NB, 32, 128, 12])
    pC = dout("pC", [2, 4, 128, 129])
    pm = dout("pm", [2, 8, 1])
    pS = dout("pS", [2, 128, 16 * 128])
    pcv = dout("pcv", [2, 128, 32 * 3])

    es = ExitStack()
    P = Prog(nc, es)
    cnt = [0]

    def sb(shape, dt=F32, name=None, const=False):
        cnt[0] += 1
        nm = "%s%d" % (name or "t", cnt[0])
        t = es.enter_context(nc.sbuf_tensor(nm, list(shape), dt))
        sl = tuple(slice(None) for _ in shape)
        return TV(t[sl], Buf(nm, const))

    X = sb([128, 8, NT], F32, "X")
    XB = sb([128, 8, NT], BF16, "XB")
    OT = sb([128, 16, NT], BF16, "OT")
    CST = sb([128, NCST], F32, "CST", const=True)
    PRM = sb([128, NPRM], F32, "PRM", const=True)
    WS = [sb([128, SLOT], BF16, "WS") for _ in range(NSLOT)]
    CST_ = [sb([128, 4, 129], F32, "Cst") for _ in range(2)]
    CB = sb([128, 4, 130], BF16, "Cb")
    SST = [sb([128, 16, 128], F32, "Sst") for _ in range(2)]
    SBF = sb([128, 16, 128], BF16, "Sbf")
    CARRY = [sb([128, 32, 3], F32, "carry") for _ in range(2)]
    MCAR = [sb([8, 1], F32, "mcar") for _ in range(2)]
    AEXP = sb([16, 2], F32, "aexp")
    GT = [sb([128, NT], F32, "g") for _ in range(10)]
    TM = sb([128, 5, 64], F32, "tm")
    TMX = sb([128, 5, 64], F32, "tmx")
    MPV = sb([16, 8], F32, "mpv")
    DEC = sb([16, 8], F32, "dec")
    DECB = sb([128, 16, 8], F32, "decb")
    M0T = sb([8, 4], F32, "m0t")
    MNEW = sb([8, 4], F32, "mnew")
    C0 = sb([128, 4, 129], F32, "C0")
    CNEW = sb([128, 4, 129], F32, "Cnew")
    S0 = [sb([128, 4, 128], F32, "S0") for _ in range(2)]
    SNEW = [sb([128, 4, 128], F32, "Snew") for _ in range(2)]
    XS = [sb([128, 4, 7], F32, "xs") for _ in range(2)]
    SCRW = 16512
    SCR = sb([128, SCRW], F32, "SCR")

    PSS = []
    for i in range(8):
        t = es.enter_context(nc.psum_tensor("ps%d" % i, [128, 512], F32))
        PSS.append(TV(t[:, :], Buf("ps%d" % i, psum=True)))
    rr = {"ps": 0}

    def psum():
        rr["ps"] += 1
        return PSS[rr["ps"] % 8]

    pbig = psum

    class Carver:
        def __init__(self):
            self.off = 0

        def tile(self, shape, dt=F32, name="c"):
            n = int(np.prod(shape[1:]))
            w = n if dt == F32 else (n + 1) // 2
            assert self.off + w <= SCRW, ("scratch overflow", self.off, w)
            ap = SCR.ap[:, self.off:self.off + w]
            self.off += w
            if dt != F32:
                ap = ap.bitcast(dt)[:, 0:n]
            if len(shape) == 3:
                ap = ap.rearrange("p (a b) -> p a b", a=shape[1])
            elif len(shape) == 4:
                ap = ap.rearrange("p (a b c) -> p a b c", a=shape[1], b=shape[2])
            if shape[0] != 128:
                ap = ap[0:shape[0]]
            return TV(ap, Buf(name))

        def ring(self, n, shape, dt=F32, name="r"):
            ts_ = [self.tile(shape, dt, name) for _ in range(n)]
            st = [0]

            def nxt():
                st[0] += 1
                return ts_[st[0] % n]
            return nxt

    ident = CST[:, C_ID:C_ID + 128]
    ones = CST[:, C_ONE:C_ONE + 128]
    MI, MS, MPo = (CST[:, C_MI:C_MI + 128], CST[:, C_MS:C_MS + 128], CST[:, C_MP:C_MP + 128])
    SMI, SMS, SMP = (CST[:, C_SMI:C_SMI + 128], CST[:, C_SMS:C_SMS + 128], CST[:, C_SMP:C_SMP + 128])
    SEL = CST[:, C_SEL:C_SEL + 2048]
    RST = CST[:, C_RST:C_RST + NT]
    BMF = CST[:, C_BMF:C_BMF + 64].re("p (s t) -> p s t", s=4)
    BMT = CST[:, C_BMT:C_BMT + 4]
    LNP = PRM[:, P_LN:P_LN + 128].re("p (l k c) -> p l k c", l=4, k=4)
    ANW = PRM[:, P_ANW:P_ANW + 16].re("p (j h) -> p j h", j=2)
    BNW = PRM[:, P_BNW:P_BNW + 32].re("p (j h) -> p j h", j=2)
    CW = PRM[:, P_CW:P_CW + 256].re("p (j c k) -> p j c k", j=2, c=32)
    PG = PRM[:, P_G:P_G + 8]

    wstate = {"i": 0}

    def wnext(key):
        ui = uidx[key]
        _, kc, n = units[ui]
        sz = kc * n
        slot = WS[wstate["i"] % NSLOT]
        wstate["i"] += 1
        src = wts[:, offs[ui]:offs[ui] + sz]
        dst = slot[:, 0:sz]
        if sz % 1024 == 0 and sz > 1024:
            P.dma("pool", dst.re("p (a b) -> p a b", b=1024), src.rearrange("p (a b) -> p a b", b=1024))
        else:
            P.dma("pool", dst, src)
        return dst.re("p (k n) -> p k n", k=kc)

    P.dma("sp", CST, cst_d)
    P.dma("sp", PRM, prm_d)
    for j in range(2):
        P.memset(CST_[j], 0.0)
        P.memset(SST[j], 0.0)
        P.memset(CARRY[j], 0.0)
        P.memset(MCAR[j], 0.0)
    for g in GT:
        P.memset(g, 0.0)
    P.memset(TM, 0.0)
    P.memset(TMX, 0.0)
    P.memset(MPV, 0.0)
    P.memset(DEC, 0.0)
    P.act(AEXP[0:16, 0:2], PG[0:16, 6:8], AF.Exp)

    def gate_transposes(qtys, R):
        for ti, (t0, n) in enumerate(TILES):
            ps = psum()
            for q, gq in enumerate(qtys):
                P.tr(ps[0:n, q * 16:q * 16 + R], gq[0:R, t0:t0 + n], ident[0:R, 0:R])
            P.copy(TM[0:n, ti, :].re("p (q h) -> p q h", q=4)[:, :, 0:R],
                   ps[0:n, 0:64].re("p (q h) -> p q h", q=4)[:, :, 0:R])

    def seg3(tv, R):
        return (tv[0:R, 0:TP].re("r (c l) -> r c l", l=128), tv[0:R, TP:NT].re("r (c l) -> r c l", l=4))

    def layer_norm(layer, which, cv):
        mean = cv.tile([128, NT], F32, "mean")
        msq = cv.tile([128, NT], F32, "msq")
        rstd = cv.tile([128, NT], F32, "rstd")
        mr = cv.tile([128, NT], F32, "mr")
        accs = cv.tile([128, NT], F32, "accs")
        accq = cv.tile([128, NT], F32, "accq")
        sqr = cv.ring(3, [128, NT], F32, "sq")
        tmpr = cv.ring(3, [128, NT], F32, "lt")
        P.tt(accs[:, :], X[:, 0, :], X[:, 1, :], ALU.add, eng="pool")
        for dc in range(2, 8):
            P.tt(accs[:, :], accs[:, :], X[:, dc, :], ALU.add, eng="pool")
        sq0 = None
        for dc in range(8):
            sq = sqr()
            P.act(sq[:, :], X[:, dc, :], AF.Square)
            if dc == 0:
                sq0 = sq
            elif dc == 1:
                P.tt(accq[:, :], sq0[:, :], sq[:, :], ALU.add)
            else:
                P.tt(accq[:, :], accq[:, :], sq[:, :], ALU.add)
        for (g0, ng) in GROUPS:
            psS = pbig()
            P.mm(psS[:, 0:ng], ones, accs[:, g0:g0 + ng])
            P.act(mean[:, g0:g0 + ng], psS[:, 0:ng], AF.Copy, scale=1.0 / D)
            psQ = pbig()
            P.mm(psQ[:, 0:ng], ones, accq[:, g0:g0 + ng])
            P.act(msq[:, g0:g0 + ng], mean[:, g0:g0 + ng], AF.Square)
            P.stt(rstd[:, g0:g0 + ng], psQ[:, 0:ng], 1.0 / D, msq[:, g0:g0 + ng], ALU.mult, ALU.subtract)
        P.ts(rstd[:, :], rstd[:, :], 1e-5, ALU.add)
        P.act(rstd[:, :], rstd[:, :], AF.Ln)
        P.act(rstd[:, :], rstd[:, :], AF.Exp, scale=-0.5)
        P.tt(mr[:, :], mean[:, :], rstd[:, :], ALU.mult)
        for dc in range(8):
            t = tmpr()
            P.tt(t[:, :], X[:, dc, :], rstd[:, :], ALU.mult)
            P.tt(t[:, :], t[:, :], mr[:, :], ALU.subtract)
            P.ts(X[:, dc, :], t[:, :], LNP[:, layer, 2 * which, dc:dc + 1], ALU.mult,
                 LNP[:, layer, 2 * which + 1, dc:dc + 1], ALU.add)
            P.copy(XB[:, dc, :], X[:, dc, :], eng="act")

    def out_proj(kind, j, nun, kc_n, dc_per):
        for u in range(nun):
            w = wnext((kind, j, u))
            for dd in range(dc_per):
                dc = u * dc_per + dd
                for (g0, ng) in GROUPS:
                    ps = pbig()
                    for kc in range(kc_n):
                        P.mm(ps[:, 0:ng], w[:, kc, dd * 128:(dd + 1) * 128], OT[:, kc, g0:g0 + ng],
                             start=kc == 0, stop=kc == kc_n - 1)
                    P.stt(X[:, dc, g0:g0 + ng], X[:, dc, g0:g0 + ng], ALPHA, ps[:, 0:ng], ALU.mult, ALU.add)

    def mlp(layer):
        P.fence()
        cv = Carver()
        HT = cv.tile([128, 32, NT], BF16, "HT")
        sqr = cv.ring(2, [128, 512], F32, "hsq")
        for u in range(8):
            w = wnext(("w1", layer, u))
            for fc in range(4):
                for (g0, ng) in GROUPS:
                    ps = pbig()
                    for kc in range(8):
                        P.mm(ps[:, 0:ng], w[:, kc, fc * 128:(fc + 1) * 128], XB[:, kc, g0:g0 + ng],
                             start=kc == 0, stop=kc == 7)
                    sq = sqr()
                    P.act(sq[:, 0:ng], ps[:, 0:ng], AF.Square)
                    P.stt(HT[:, u * 4 + fc, g0:g0 + ng], ps[:, 0:ng], 0.0, sq[:, 0:ng], ALU.is_gt, ALU.mult)
        for dc in range(8):
            w = wnext(("w2", layer, dc))
            for (g0, ng) in GROUPS:
                ps = pbig()
                for fc in range(32):
                    P.mm(ps[:, 0:ng], w[:, fc, 0:128], HT[:, fc, g0:g0 + ng], start=fc == 0, stop=fc == 31)
                P.stt(X[:, dc, g0:g0 + ng], X[:, dc, g0:g0 + ng], ALPHA, ps[:, 0:ng], ALU.mult, ALU.add)
        P.fence()
        layer_norm(layer, 1, Carver())

    def run_sched(slot_fns, nslots, spawn):
        active = []
        free = list(range(nslots))
        pend = list(slot_fns)
        while pend or active or spawn:
            while pend and free:
                sl = free.pop(0)
                active.append([pend.pop(0)(sl), sl])
            while spawn:
                active.append([spawn.pop(0), None])
            for item in list(active):
                try:
                    next(item[0])
                except StopIteration:
                    active.remove(item)
                    if item[1] is not None:
                        free.append(item[1])

    def mlstm(blk, j):
        P.fence()
        cv = Carver()
        gi, lf, mt, bb, aa, negM, wi, enm, wk, gtmp = GT
        Cst = CST_[j]
        R = 8
        wg = wnext(("aG", j))
        for (g0, ng) in GROUPS:
            for which, gt in ((0, gi), (1, lf)):
                ps = pbig()
                for kc in range(8):
                    P.mm(ps[0:8, 0:ng], wg[:, kc, which * 8:which * 8 + 8], XB[:, kc, g0:g0 + ng],
                         start=kc == 0, stop=kc == 7)
                P.ts(gt[0:8, g0:g0 + ng], ps[0:8, 0:ng], PG[0:8, 2 * which + j:2 * which + j + 1], ALU.add)
        P.act(gi[0:8, :], gi[0:8, :], AF.Tanh, scale=1.0 / 15)
        P.ts(gi[0:8, :], gi[0:8, :], 15.0, ALU.mult)
        P.act(lf[0:8, :], lf[0:8, :], AF.Tanh, scale=1.0 / 15)
        P.act(lf[0:8, :], lf[0:8, :], AF.Exp, scale=-15.0)
        P.ts(lf[0:8, :], lf[0:8, :], 1.0, ALU.add)
        P.act(lf[0:8, :], lf[0:8, :], AF.Ln)
        P.ts(lf[0:8, :], lf[0:8, :], -1.0, ALU.mult)
        if DBG < 1:
            return
        P.dma("sp", M0T, min_d[j, blk])
        P.copy(MPV[0:8, 0:1], MCAR[j][0:8, 0:1])
        P.scan(mt[0:8, 0:TP], lf[0:8, 0:TP], gi[0:8, 0:TP], MCAR[j][0:8, 0:1], ALU.add, ALU.max)
        if DBG < 1.2:
            return
        for s in range(4):
            c0 = TP + 4 * s
            P.scan(mt[0:8, c0:c0 + 4], lf[0:8, c0:c0 + 4], gi[0:8, c0:c0 + 4], M0T[0:8, s:s + 1], ALU.add, ALU.max)
        P.scan(bb[0:8, :], RST[0:8, :], lf[0:8, :], 0.0, ALU.mult, ALU.add)
        if DBG < 1.4:
            return
        P.tt(aa[0:8, :], gi[0:8, :], bb[0:8, :], ALU.subtract)
        P.tt(negM[0:8, :], bb[0:8, :], mt[0:8, :], ALU.subtract)
        mtp, mts = seg3(mt, R)
        P.copy(MPV[0:8, 1:4], mtp[:, 0:3, 127])
        P.copy(MPV[0:8, 4:8], M0T[0:8, 0:4])
        P.copy(MCAR[j][0:8, 0:1], mt[0:8, TP - 1:TP])
        P.copy(MNEW[0:8, 0:4], mts[:, :, 3])
        P.dma("sp", mout[j, blk], MNEW)
        if DBG < 1.5:
            return
        nMp, nMs = seg3(negM, R)
        wip, wis = seg3(wi, R)
        P.tt(wip, nMp, MPV[0:8, 0:4].un(2).bc([8, 4, 128]), ALU.add)
        P.tt(wis, nMs, MPV[0:8, 4:8].un(2).bc([8, 4, 4]), ALU.add)
        P.act(wi[0:8, :], wi[0:8, :], AF.Exp)
        P.act(enm[0:8, :], mt[0:8, :], AF.Exp, scale=-1.0)
        if DBG < 1.6:
            return
        ap_, as_ = seg3(aa, R)
        wkp, wks = seg3(wk, R)
        P.tt(wkp, ap_, nMp[:, :, 127:128].bc([8, 4, 128]), ALU.add)
        P.tt(wks, as_, nMs[:, :, 3:4].bc([8, 4, 4]), ALU.add)
        P.act(wk[0:8, :], wk[0:8, :], AF.Exp)
        P.copy(DEC[0:8, 0:4], wip[:, :, 127])
        P.copy(DEC[0:8, 4:8], wis[:, :, 3])
        if DBG < 1.7:
            return
        for h in range(8):
            ps = psum()
            P.mm(ps[:, 0:8], SEL[0:8, h * 128:(h + 1) * 128], DEC[0:8, 0:8])
            P.copy(DECB[:, h, :], ps[:, 0:8])
        if DBG < 2:
            return
        gate_transposes([aa, wi, enm, wk], 8)
        if DBG < 3:
            return
        TMv = TM.re("p t (q h) -> p t q h", q=4)

        qT = cv.tile([128, NT], BF16, "qT")
        kT = cv.tile([128, NT], BF16, "kT")
        ktm = cv.tile([128, 5, 128], BF16, "ktm")
        vaug = cv.tile([128, 5, 2, 130], BF16, "vaug")
        og = cv.tile([128, 2, NT], BF16, "og")
        slotA = [cv.tile([128, 128], F32, "argA") for _ in range(8)]
        ST = cv.tile([128, 10, 128], BF16, "ST")
        hun_r = [cv.ring(3, [128, 129], F32, "hun") for _ in range(2)]
        tmp_r = [cv.ring(2, [128, 129], F32, "tmp") for _ in range(2)]
        hs_r = [cv.ring(2, [128, 128], F32, "hs") for _ in range(2)]
        kw_r = [cv.ring(2, [128, 64], BF16, "kw") for _ in range(2)]
        kwm = [cv.tile([16, 4, 64], BF16, "kwm") for _ in range(2)]
        qTm = cv.tile([128, 4, 16], F32, "qTm")
        sm_r = [cv.ring(5, [128, 16], F32, "sm") for _ in range(2)]
        junk = [cv.tile([128, 128], F32, "junk") for _ in range(2)]
        P.memset(vaug, 1.0)

        for p in range(4):
            wa = wnext(("aA", j, p))
            wb = wnext(("aB", j, p))
            if DBG < 3.05:
                continue
            for (g0, ng) in GROUPS:
                ps = pbig()
                for kc in range(8):
                    P.mm(ps[:, 0:ng], wa[:, kc, 0:128], XB[:, kc, g0:g0 + ng], start=kc == 0, stop=kc == 7)
                if DBG < 3.06:
                    continue
                P.copy(qT[:, g0:g0 + ng], ps[:, 0:ng], eng="act")
                if DBG < 3.07:
                    continue
                ps = pbig()
                for kc in range(8):
                    P.mm(ps[:, 0:ng], wa[:, kc, 128:256], XB[:, kc, g0:g0 + ng], start=kc == 0, stop=kc == 7)
                if KVAR == 1:
                    P.copy(kT[:, g0:g0 + ng], ps[:, 0:ng], eng="act")
                elif KVAR == 2:
                    P.ts(kT[:, g0:g0 + ng], ps[:, 0:ng], 0.125, ALU.mult)
                elif KVAR == 3:
                    P.act(kT[:, g0:g0 + ng], ps[:, 0:ng], AF.Identity, scale=0.125)
                else:
                    P.act(kT[:, g0:g0 + ng], ps[:, 0:ng], AF.Copy, scale=0.125)
            if DBG < 3.1:
                continue
            for ti, (t0, n) in enumerate(TILES):
                ps = pbig()
                for kc in range(8):
                    P.mm(ps[0:n, 0:384], XB[:, kc, t0:t0 + n], wa[:, kc, 128:512], start=kc == 0, stop=kc == 7)
                if DBG < 3.12:
                    continue
                P.act(ktm[0:n, ti, :], ps[0:n, 0:128], AF.Copy, scale=0.125)
                if DBG < 3.13:
                    continue
                P.copy(vaug[0:n, ti, :, 0:128], ps[0:n, 128:384].re("p (e d) -> p e d", e=2), eng=("act" if KVAR == 5 else "dve"))
            if DBG < 3.2:
                continue
            for e in range(2):
                for (g0, ng) in GROUPS:
                    ps = pbig()
                    for kc in range(8):
                        P.mm(ps[:, 0:ng], wb[:, kc, e * 128:(e + 1) * 128], XB[:, kc, g0:g0 + ng],
                             start=kc == 0, stop=kc == 7)
                    P.act(og[:, e, g0:g0 + ng], ps[:, 0:ng], AF.Sigmoid)
            if DBG < 3.3:
                continue
            P.dma("sp", C0.re("p s d -> p (s d)"), cin[j, blk, p])
            if DBG < 3.4:
                continue
            for e in range(2):
                P.copy(CB[64 * e:64 * e + 64, p, 0:129], Cst[64 * e:64 * e + 64, p, :], eng="act")
                P.tt(qTm[64 * e:64 * e + 64, :, :], qT[64 * e:64 * e + 64, TP:NT].un(1).bc([64, 4, 16]),
                     BMF[64 * e:64 * e + 64, :, :], ALU.mult)

            done = {}
            spawn = []

            def intra_chain(ti, e, sl, p=p):
                t0, n = TILES[ti]
                sample = ti == 4
                h = 2 * p + e
                r0 = 64 * e
                idx = ti * 2 + e
                arg = slotA[sl]
                ps2 = psum()
                P.mm(ps2[0:n, 0:n], SEL[0:8, h * 128:h * 128 + n], negM[0:8, t0:t0 + n])
                P.tt(arg[0:n, 0:n], ps2[0:n, 0:n], (SMI if sample else MI)[0:n, 0:n], ALU.add)
                yield
                P.act(arg[0:n, 0:n], arg[0:n, 0:n], AF.Exp, bias=TMv[0:n, ti, 0, h:h + 1])
                yield
                ps1 = psum()
                P.mm(ps1[0:n, 0:n], kT[r0:r0 + 64, t0:t0 + n], qT[r0:r0 + 64, t0:t0 + n])
                P.tt(ST[0:n, idx, 0:n], ps1[0:n, 0:n], arg[0:n, 0:n], ALU.mult)
                done[(ti, e)] = True

            def out_chain(ti, e, hun, p=p):
                t0, n = TILES[ti]
                h = 2 * p + e
                sm = sm_r[e]()
                P.act(sm[0:n, 0:1], hun[0:n, 128:129], AF.Abs)
                yield
                P.tt(sm[0:n, 0:1], sm[0:n, 0:1], TMv[0:n, ti, 2, h:h + 1], ALU.max)
                yield
                P.recip(sm[0:n, 1:2], sm[0:n, 0:1])
                yield
                P.act(junk[e][0:n, :], hun[0:n, 0:128], AF.Square, scale=sm[0:n, 1:2], accum=sm[0:n, 2:3])
                yield
                P.ts(sm[0:n, 3:4], sm[0:n, 2:3], 1.0 / 128, ALU.mult, 1e-6, ALU.add)
                yield
                P.act(sm[0:n, 3:4], sm[0:n, 3:4], AF.Ln)
                yield
                P.act(sm[0:n, 3:4], sm[0:n, 3:4], AF.Exp, scale=-0.5)
                yield
                P.tt(sm[0:n, 4:5], sm[0:n, 3:4], sm[0:n, 1:2], ALU.mult)
                yield
                hs = hs_r[e]()
                P.act(hs[0:n, :], hun[0:n, 0:128], AF.Copy, scale=sm[0:n, 4:5])
                yield
                ps5 = psum()
                P.tr(ps5[:, 0:n], hs[0:n, :], ident[0:n, 0:n])
                P.stt(OT[:, h, t0:t0 + n], ps5[:, 0:n], ANW[:, j, h:h + 1], og[:, e, t0:t0 + n],
                      ALU.mult, ALU.mult)

            def rec_chain(e, p=p):
                h = 2 * p + e
                r0 = 64 * e
                for ti, (t0, n) in enumerate(TILES):
                    sample = ti == 4
                    idx = ti * 2 + e
                    while not done.get((ti, e)):
                        yield
                    ps3 = psum()
                    P.mm(ps3[0:n, 0:129], ST[0:n, idx, 0:n], vaug[0:n, ti, e, 0:129])
                    ps4 = psum()
                    if not sample:
                        P.mm(ps4[0:n, 0:129], qT[r0:r0 + 64, t0:t0 + n], CB[r0:r0 + 64, p, 0:129])
                    else:
                        for s_ in range(4):
                            P.mm(ps4[0:n, 0:129], qTm[r0:r0 + 64, s_, :], C0[r0:r0 + 64, s_, :],
                                 start=s_ == 0, stop=s_ == 3)
                    tmp = tmp_r[e]()
                    P.act(tmp[0:n, :], ps4[0:n, 0:129], AF.Copy, scale=TMv[0:n, ti, 1, h:h + 1])
                    hun = hun_r[e]()
                    P.tt(hun[0:n, :], ps3[0:n, 0:129], tmp[0:n, :], ALU.add)
                    spawn.append(out_chain(ti, e, hun))
                    yield
                    kw = kw_r[e]()
                    P.ts(kw[0:n, :], ktm[0:n, ti, r0:r0 + 64], TMv[0:n, ti, 3, h:h + 1], ALU.mult)
                    yield
                    if not sample:
                        ps6 = psum()
                        P.mm(ps6[r0:r0 + 64, 0:129], kw[0:n, :], vaug[0:n, ti, e, 0:129])
                        P.stt(Cst[r0:r0 + 64, p, :], Cst[r0:r0 + 64, p, :], DECB[r0:r0 + 64, h, ti:ti + 1],
                              ps6[r0:r0 + 64, 0:129], ALU.mult, ALU.add)
                        yield
                        P.copy(CB[r0:r0 + 64, p, 0:129], Cst[r0:r0 + 64, p, :], eng="act")
                    else:
                        for s_ in range(4):
                            P.ts(kwm[e][0:16, s_, :], kw[0:16, :], BMT[0:16, s_:s_ + 1], ALU.mult)
                        yield
                        for s_ in range(4):
                            ps7 = psum()
                            P.mm(ps7[r0:r0 + 64, 0:129], kwm[e][0:16, s_, :], vaug[0:16, ti, e, 0:129])
                            P.stt(CNEW[r0:r0 + 64, s_, :], C0[r0:r0 + 64, s_, :], DECB[r0:r0 + 64, h, 4 + s_:5 + s_],
                                  ps7[r0:r0 + 64, 0:129], ALU.mult, ALU.add)
                    yield

            spawn.extend([rec_chain(0), rec_chain(1)])
            run_sched([(lambda sl, ti=ti, e=e: intra_chain(ti, e, sl)) for ti in range(5) for e in range(2)], 8, spawn)
            P.dma("sp", cout[j, blk, p], CNEW.re("p s d -> p (s d)"))
        if blk == nblocks - 1:
            for p in range(4):
                P.dma("sp", pC[j, p], Cst[:, p, :])
            P.dma("sp", pm[j], MCAR[j])

    def gdn(blk, j):
        P.fence()
        cv = Carver()
        lb, be, gg, G, GL, kda, gtmp = GT[0:7]
        Sst = SST[j]
        carry = CARRY[j]
        R = 16
        wg = wnext(("bG", j))
        for (g0, ng) in GROUPS:
            for which, gt in ((0, lb), (1, gg)):
                ps = pbig()
                for kc in range(8):
                    P.mm(ps[0:16, 0:ng], wg[:, kc, which * 16:which * 16 + 16], XB[:, kc, g0:g0 + ng],
                         start=kc == 0, stop=kc == 7)
                if which == 0:
                    P.act(lb[0:16, g0:g0 + ng], ps[0:16, 0:ng], AF.Exp, scale=-1.0)
                else:
                    P.ts(gg[0:16, g0:g0 + ng], ps[0:16, 0:ng], PG[0:16, 4 + j:5 + j], ALU.add)
        P.ts(lb[0:16, :], lb[0:16, :], 1.0, ALU.add)
        P.act(lb[0:16, :], lb[0:16, :], AF.Ln)
        P.act(be[0:16, :], lb[0:16, :], AF.Exp, scale=-1.0)
        P.act(gg[0:16, :], gg[0:16, :], AF.Exp)
        P.ts(gg[0:16, :], gg[0:16, :], 1.0, ALU.add)
        P.act(gg[0:16, :], gg[0:16, :], AF.Ln)
        P.ts(gg[0:16, :], gg[0:16, :], AEXP[0:16, j:j + 1], ALU.mult, -1.0, ALU.mult)
        P.scan(G[0:16, :], RST[0:16, :], gg[0:16, :], 0.0, ALU.mult, ALU.add)
        P.tt(GL[0:16, :], G[0:16, :], lb[0:16, :], ALU.subtract)
        Gp, Gs = seg3(G, R)
        kp, ks = seg3(kda, R)
        P.tt(kp, Gp[:, :, 127:128].bc([16, 4, 128]), Gp, ALU.subtract)
        P.tt(ks, Gs[:, :, 3:4].bc([16, 4, 4]), Gs, ALU.subtract)
        P.copy(DEC[0:16, 0:4], Gp[:, :, 127])
        P.copy(DEC[0:16, 4:8], Gs[:, :, 3])
        for hq in range(4):
            ps = psum()
            for hh in range(4):
                h = hq * 4 + hh
                P.mm(ps[:, hh * 8:hh * 8 + 8], SEL[0:16, h * 128:(h + 1) * 128], DEC[0:16, 0:8])
            P.act(DECB[:, hq * 4:hq * 4 + 4, :], ps[:, 0:32].re("p (h s) -> p h s", h=4), AF.Exp)
        gate_transposes([G, GL, be, kda], 16)
        TMv = TM.re("p t (q h) -> p t q h", q=4)
        TXv = TMX.re("p t (q h) -> p t q h", q=4)
        P.ts(TXv[:, :, 0, :], TMv[:, :, 0, :], -1.0, ALU.mult)
        P.act(TXv[:, :, 1, :], TMv[:, :, 0, :], AF.Exp)
        P.act(TXv[:, :, 2, :], TMv[:, :, 1, :], AF.Exp)
        P.act(TXv[:, :, 3, :], TMv[:, :, 3, :], AF.Exp)
        for h in range(16):
            P.copy(SBF[:, h, :], Sst[:, h, :], eng="act")

        xpre_r = cv.ring(2, [128, 3 + TP], F32, "xpre")
        cc_r = cv.ring(2, [128, NT], F32, "cc")
        sqt = [cv.tile([128, NT], F32, "sq") for _ in range(2)]
        rnt = [cv.tile([128, NT], F32, "rn") for _ in range(2)]
        qT = cv.tile([128, NT], BF16, "gqT")
        kT = cv.tile([128, NT], BF16, "gkT")
        qf = cv.tile([128, TS], F32, "qf")
        kbg = cv.tile([128, 5, 2, 128], BF16, "kbg")
        kdec = cv.tile([128, 5, 2, 128], BF16, "kdec")
        vb = cv.tile([128, 5, 2, 128], BF16, "vb")
        zs = cv.tile([128, 2, NT], BF16, "zs")
        WCH = 4
        slot_t = [[cv.tile([128, 128], F32, "sl") for _ in range(7)] for _ in range(WCH)]
        AT = cv.tile([128, 10, 128], BF16, "AT")
        PBT = cv.tile([128, 10, 128], BF16, "PBT")
        WTN = cv.tile([128, 10, 128], BF16, "WTN")
        kk_r = cv.ring(6, [128, 128], F32, "kk")
        wTf = [cv.tile([128, 16], F32, "wTf") for _ in range(2)]
        wTm = [cv.tile([128, 4, 16], F32, "wTm") for _ in range(2)]
        qTm = cv.tile([128, 4, 16], F32, "qTm")
        kdm = [cv.tile([16, 4, 128], BF16, "kdm") for _ in range(2)]
        vn_r = [cv.ring(2, [128, 128], BF16, "vn") for _ in range(2)]
        tmp_r = [cv.ring(2, [128, 128], F32, "gtmp") for _ in range(2)]
        oall_r = [cv.ring(3, [128, 128], F32, "oall") for _ in range(2)]
        os_r = [cv.ring(2, [128, 128], F32, "os") for _ in range(2)]
        sm_r = [cv.ring(3, [128, 8], F32, "gsm") for _ in range(2)]
        junk1 = cv.tile([128, 128], F32, "gjunk")
        junk = [junk1, junk1]

        def conv_panel_g(w, wc0, panel, xsb, res):
            xpre = xpre_r()
            P.copy(xpre[:, 0:3], carry[:, panel, :])
            P.dma("sp", xsb[:, :, 0:3], cvin[j, blk, panel].rearrange("p (s r) -> p s r", r=3))
            for (g0, ng) in GROUPS:
                ps = pbig()
                for kc in range(8):
                    P.mm(ps[:, 0:ng], w[:, kc, wc0:wc0 + 128], XB[:, kc, g0:g0 + ng], start=kc == 0, stop=kc == 7)
                if g0 == 0:
                    P.copy(xpre[:, 3:3 + TP], ps[:, 0:TP], eng="act")
                else:
                    P.copy(xsb[:, :, 3:7], ps[:, 0:TS].re("p (s t) -> p s t", t=4), eng="act")
            yield
            cc = cc_r()
            res["cc"] = cc
            ccs = cc[:, TP:NT].re("p (s t) -> p s t", t=4)
            P.ts(cc[:, 0:TP], xpre[:, 0:TP], CW[:, j, panel, 0:1], ALU.mult)
            P.ts(ccs, xsb[:, :, 0:4], CW[:, j, panel, 0:1], ALU.mult)
            yield
            for k in range(1, 4):
                P.stt(cc[:, 0:TP], xpre[:, k:k + TP], CW[:, j, panel, k:k + 1], cc[:, 0:TP], ALU.mult, ALU.add)
                P.stt(ccs, xsb[:, :, k:k + 4], CW[:, j, panel, k:k + 1], ccs, ALU.mult, ALU.add)
                yield
            P.copy(carry[:, panel, :], xpre[:, TP:TP + 3])
            P.dma("sp", cvout[j, blk, panel].rearrange("p (s r) -> p s r", r=3), xsb[:, :, 4:7])
            P.act(cc[:, :], cc[:, :], AF.Silu)
            yield

        def conv_panel(w, wc0, panel, xsb):
            res = {}
            for _ in conv_panel_g(w, wc0, panel, xsb, res):
                pass
            return res["cc"]

        xsi = [0]

        def nxs():
            xsi[0] += 1
            return XS[xsi[0] % 2]

        nxt = {}
        for g in range(8):
            def qk_chain(pi, g, wq, gate):
                scl = 128.0 ** -0.5
                panel = g if pi == 0 else 8 + g
                res = {}
                for _ in conv_panel_g(wq, pi * 128, panel, nxs(), res):
                    yield
                cc = res["cc"]
                sq_, rn_ = sqt[pi], rnt[pi]
                P.act(sq_[:, :], cc[:, :], AF.Square)
                yield
                for (g0, ng) in GROUPS:
                    ps = pbig()
                    P.mm(ps[:, 0:ng], ones, sq_[:, g0:g0 + ng])
                    P.ts(rn_[:, g0:g0 + ng], ps[:, 0:ng], 1e-6, ALU.add)
                yield
                P.act(rn_[:, :], rn_[:, :], AF.Ln)
                yield
                P.act(rn_[:, :], rn_[:, :], AF.Exp, scale=-0.5)
                yield
                if pi == 0:
                    while not gate():
                        yield
                    P.stt(qT[:, :], cc[:, :], scl, rn_[:, :], ALU.mult, ALU.mult)
                    P.stt(qf[:, :], cc[:, TP:NT], scl, rn_[:, TP:NT], ALU.mult, ALU.mult)
                else:
                    P.tt(cc[:, :], cc[:, :], rn_[:, :], ALU.mult)
                    yield
                    while not gate():
                        yield
                    P.copy(kT[:, :], cc[:, :], eng="act")
                    for ti, (t0, n) in enumerate(TILES):
                        ps = psum()
                        P.tr(ps[0:n, 0:128], cc[:, t0:t0 + n], ident)
                        for e in range(2):
                            h = 2 * g + e
                            P.act(kbg[0:n, ti, e, :], ps[0:n, 0:128], AF.Copy, scale=TXv[0:n, ti, 2, h:h + 1])
                            P.ts(kdec[0:n, ti, e, :], ps[0:n, 0:128], TXv[0:n, ti, 3, h:h + 1], ALU.mult)
                        yield

            if g == 0:
                wq = wnext(("bQ", j, 0))
                run_sched([], 0, [qk_chain(0, 0, wq, lambda: True), qk_chain(1, 0, wq, lambda: True)])
            else:
                wq = nxt["wq"]
            wz = wnext(("bZ", j, g))
            cnt = [10]

            def tracked(gen, cnt=cnt):
                cnt[0] += 1

                def w():
                    yield from gen
                    cnt[0] -= 1
                return w()

            def solve_tracked(ti, e, sl, cnt=cnt):
                yield from solve_chain(ti, e, sl)
                cnt[0] -= 1
            vdone = {}
            zdone = {}

            def v_chain(e, g=g, wq=wq):
                h = 2 * g + e
                res = {}
                for _ in conv_panel_g(wq, 256 + e * 128, 16 + h, nxs(), res):
                    yield
                cc = res["cc"]
                for ti, (t0, n) in enumerate(TILES):
                    ps = psum()
                    P.tr(ps[0:n, 0:128], cc[:, t0:t0 + n], ident)
                    P.ts(vb[0:n, ti, e, :], ps[0:n, 0:128], TMv[0:n, ti, 2, h:h + 1], ALU.mult)
                    yield
                vdone[e] = True

            def z_chain(e, wz=wz):
                for (g0, ng) in GROUPS:
                    ps = pbig()
                    for kc in range(8):
                        P.mm(ps[:, 0:ng], wz[:, kc, e * 128:(e + 1) * 128], XB[:, kc, g0:g0 + ng],
                             start=kc == 0, stop=kc == 7)
                    P.act(zs[:, e, g0:g0 + ng], ps[:, 0:ng], AF.Silu)
                    yield
                zdone[e] = True

            P.tt(qTm[:, :, :], qf[:, :].un(1).bc([128, 4, 16]), BMF, ALU.mult)
            S0e, SNe = [], []
            for e in range(2):
                h = 2 * g + e
                P.dma("sp", S0[e].re("p s d -> p (s d)"), sin[j, blk, h])
                S0e.append(S0[e])
                SNe.append(SNEW[e])

            kkt = {}

            def solve_chain(ti, e, sl):
                t0, n = TILES[ti]
                sample = ti == 4
                nlev = 2 if sample else 7
                h = 2 * g + e
                idx = ti * 2 + e
                tB, tM, tD, tP, u0, u1, u2 = slot_t[sl]
                if e == 0:
                    psKKp = psum()
                    P.mm(psKKp[0:n, 0:n], kT[:, t0:t0 + n], kT[:, t0:t0 + n])
                    KKs = kk_r()
                    P.copy(KKs[0:n, 0:n], psKKp[0:n, 0:n], eng="dve")
                    psQKp = psum()
                    P.mm(psQKp[0:n, 0:n], kT[:, t0:t0 + n], qT[:, t0:t0 + n])
                    QKs = kk_r()
                    P.copy(QKs[0:n, 0:n], psQKp[0:n, 0:n], eng="act")
                    kkt[ti] = (KKs, QKs)
                KKs, QKs = kkt[ti]
                psG = psum()
                P.mm(psG[0:n, 0:n], SEL[0:16, h * 128:h * 128 + n], G[0:16, t0:t0 + n])
                P.tt(tM[0:n, 0:n], psG[0:n, 0:n], (SMP if sample else MPo)[0:n, 0:n], ALU.add)
                P.tt(tD[0:n, 0:n], psG[0:n, 0:n], (SMI if sample else MI)[0:n, 0:n], ALU.add)
                psGL = psum()
                P.mm(psGL[0:n, 0:n], SEL[0:16, h * 128:h * 128 + n], GL[0:16, t0:t0 + n])
                P.tt(tB[0:n, 0:n], psGL[0:n, 0:n], (SMS if sample else MS)[0:n, 0:n], ALU.add)
                yield
                P.act(tB[0:n, 0:n], tB[0:n, 0:n], AF.Exp, bias=TXv[0:n, ti, 0, h:h + 1])
                P.act(tM[0:n, 0:n], tM[0:n, 0:n], AF.Exp, bias=TMv[0:n, ti, 1, h:h + 1], scale=-1.0)
                P.act(tD[0:n, 0:n], tD[0:n, 0:n], AF.Exp, bias=TXv[0:n, ti, 0, h:h + 1])
                yield
                P.tt(tB[0:n, 0:n], KKs[0:n, 0:n], tB[0:n, 0:n], ALU.mult, eng="pool")
                P.tt(tM[0:n, 0:n], KKs[0:n, 0:n], tM[0:n, 0:n], ALU.mult, eng="pool")
                P.tt(AT[0:n, idx, 0:n], QKs[0:n, 0:n], tD[0:n, 0:n], ALU.mult, eng="pool")
                P.tt(tP[0:n, 0:n], ident[0:n, 0:n], tB[0:n, 0:n], ALU.subtract, eng="pool")
                yield
                Nc, Mc, Pc = tB, tM, tP
                spare = [u0, u1, u2]
                for lev in range(1, nlev):
                    M2, N2, Pn = spare
                    psM = psum()
                    P.mm(psM[0:n, 0:n], Nc[0:n, 0:n], Mc[0:n, 0:n])
                    P.copy(M2[0:n, 0:n], psM[0:n, 0:n], eng="act")
                    yield
                    psP = psum()
                    P.mm(psP[0:n, 0:n], M2[0:n, 0:n], Pc[0:n, 0:n])
                    P.tt(Pn[0:n, 0:n], psP[0:n, 0:n], Pc[0:n, 0:n], ALU.add)
                    if lev < nlev - 1:
                        psN = psum()
                        P.tr(psN[0:n, 0:n], M2[0:n, 0:n], ident[0:n, 0:n])
                        P.copy(N2[0:n, 0:n], psN[0:n, 0:n], eng="dve")
                    yield
                    spare = [Mc, Nc, Pc]
                    Pc, Mc = Pn, M2
                    if lev < nlev - 1:
                        Nc = N2
                    else:
                        spare[1] = N2
                P.copy(PBT[0:n, idx, 0:n], Pc[0:n, 0:n], eng="act")
                yield
                psw = psum()
                P.mm(psw[:, 0:n], kbg[0:n, ti, e, :], PBT[0:n, idx, 0:n])
                if not sample:
                    P.act(WTN[:, idx, 0:n], psw[:, 0:n], AF.Copy, scale=-1.0)
                else:
                    P.act(wTf[e][:, 0:n], psw[:, 0:n], AF.Copy, scale=-1.0)
                    yield
                    P.tt(wTm[e][:, :, :], wTf[e][:, :].un(1).bc([128, 4, 16]), BMF, ALU.mult)
                done[(ti, e)] = True

            done = {}
            spawn = []

            def out_chain(ti, e, oall):
                t0, n = TILES[ti]
                h = 2 * g + e
                sm = sm_r[e]()
                P.act(junk[e][0:n, :], oall[0:n, :], AF.Square, accum=sm[0:n, 0:1])
                yield
                while not zdone.get(e):
                    yield
                P.ts(sm[0:n, 1:2], sm[0:n, 0:1], 1.0 / 128, ALU.mult, 1e-6, ALU.add)
                yield
                P.act(sm[0:n, 1:2], sm[0:n, 1:2], AF.Ln)
                yield
                P.act(sm[0:n, 1:2], sm[0:n, 1:2], AF.Exp, scale=-0.5)
                yield
                osb = os_r[e]()
                P.act(osb[0:n, :], oall[0:n, :], AF.Copy, scale=sm[0:n, 1:2])
                yield
                ps5 = psum()
                P.tr(ps5[:, 0:n], osb[0:n, :], ident[0:n, 0:n])
                P.stt(OT[:, h, t0:t0 + n], ps5[:, 0:n], BNW[:, j, h:h + 1], zs[:, e, t0:t0 + n],
                      ALU.mult, ALU.mult)

            def rec_chain(e):
                h = 2 * g + e
                for ti, (t0, n) in enumerate(TILES):
                    sample = ti == 4
                    idx = ti * 2 + e
                    while not (done.get((ti, e)) and vdone.get(e)):
                        yield
                    psv = psum()
                    P.mm(psv[0:n, 0:128], PBT[0:n, idx, 0:n], vb[0:n, ti, e, :], start=True, stop=False)
                    if not sample:
                        P.mm(psv[0:n, 0:128], WTN[:, idx, 0:n], SBF[:, h, :], start=False, stop=True)
                    else:
                        for s_ in range(4):
                            P.mm(psv[0:n, 0:128], wTm[e][:, s_, :], S0e[e][:, s_, :], start=False, stop=s_ == 3)
                    vn = vn_r[e]()
                    P.copy(vn[0:n, :], psv[0:n, 0:128], eng="act")
                    yield
                    pso1 = psum()
                    if not sample:
                        P.mm(pso1[0:n, 0:128], qT[:, t0:t0 + n], SBF[:, h, :])
                    else:
                        for s_ in range(4):
                            P.mm(pso1[0:n, 0:128], qTm[:, s_, :], S0e[e][:, s_, :], start=s_ == 0, stop=s_ == 3)
                    tmp = tmp_r[e]()
                    P.act(tmp[0:n, :], pso1[0:n, 0:128], AF.Copy, scale=TXv[0:n, ti, 1, h:h + 1])
                    pso2 = psum()
                    P.mm(pso2[0:n, 0:128], AT[0:n, idx, 0:n], vn[0:n, :])
                    oall = oall_r[e]()
                    P.tt(oall[0:n, :], pso2[0:n, 0:128], tmp[0:n, :], ALU.add)
                    if not sample:
                        psS = psum()
                        P.mm(psS[:, 0:128], kdec[0:n, ti, e, :], vn[0:n, :])
                        P.stt(Sst[:, h, :], Sst[:, h, :], DECB[:, h, ti:ti + 1], psS[:, 0:128], ALU.mult, ALU.add)
                        P.copy(SBF[:, h, :], Sst[:, h, :], eng="dve")
                    else:
                        for s_ in range(4):
                            P.ts(kdm[e][0:16, s_, :], kdec[0:16, ti, e, :], BMT[0:16, s_:s_ + 1], ALU.mult)
                        for s_ in range(4):
                            psS = psum()
                            P.mm(psS[:, 0:128], kdm[e][0:16, s_, :], vn[0:16, :])
                            P.stt(SNe[e][:, s_, :], S0e[e][:, s_, :], DECB[:, h, 4 + s_:5 + s_], psS[:, 0:128],
                                  ALU.mult, ALU.add)
                    spawn.append(tracked(out_chain(ti, e, oall)))
                    yield

            def next_front(g=g, vdone=None, cnt=cnt):
                while not (vdone_ref.get(0) and vdone_ref.get(1)):
                    yield
                wqn = wnext(("bQ", j, g + 1))
                nxt["wq"] = wqn
                gate = lambda: cnt[0] == 0
                alive = [qk_chain(0, g + 1, wqn, gate), qk_chain(1, g + 1, wqn, gate)]
                while alive:
                    for c in list(alive):
                        try:
                            next(c)
                        except StopIteration:
                            alive.remove(c)
                    yield

            vdone_ref = vdone
            spawn.extend([tracked(v_chain(0)), tracked(v_chain(1)), tracked(rec_chain(0)), tracked(rec_chain(1)),
                          tracked(z_chain(0)), tracked(z_chain(1))])
            if g < 7:
                spawn.append(next_front())
            run_sched([(lambda sl, ti=ti, e=e: solve_tracked(ti, e, sl)) for ti in range(5) for e in range(2)], WCH, spawn)
            for e in range(2):
                h = 2 * g + e
                P.dma("sp", sout[j, blk, h], SNe[e].re("p s d -> p (s d)"))
        if blk == nblocks - 1:
            P.dma("sp", pS[j], Sst.re("p h d -> p (h d)"))
            P.dma("sp", pcv[j], carry.re("p c r -> p (c r)"))

    for blk in range(nblocks):
        P.dma("sp", X, xin[:, :, blk * NT:(blk + 1) * NT])
        for dc in range(8):
            P.copy(XB[:, dc, :], X[:, dc, :], eng="act")
        for layer in range(nlayers):
            j = layer // 2
            if layer % 2 == 0:
                mlstm(blk, j)
                if DBG < 10:
                    continue
                P.fence()
                out_proj("aO", j, 2, 8, 4)
            else:
                gdn(blk, j)
                P.fence()
                out_proj("bO", j, 4, 16, 2)
            if DBG < 11:
                continue
            P.fence()
            layer_norm(layer, 0, Carver())
            if DBG < 12:
                continue
            mlp(layer)
        P.dma("sp", yout[:, :, blk * NT:(blk + 1) * NT], X)
    P.emit(None)
    es.close()
    return nc


_CACHE = {}


def _prep_shared(inp):
    units, offs, wtot = unit_plan()
    wts = np.empty((128, wtot), np.float32)
    for (key, kc, n), o in zip(units, offs):
        wts[:, o:o + kc * n] = _unit_data(key, inp)
    prm = np.zeros((128, NPRM), np.float32)
    ln = np.stack([inp["ln1_g"], inp["ln1_b"], inp["ln2_g"], inp["ln2_b"]], axis=1)
    prm[:, P_LN:P_LN + 128] = ln.reshape(4, 4, 8, 128).transpose(3, 0, 1, 2).reshape(128, 128)
    prm[:, P_ANW:P_ANW + 16] = inp["a_norm_w"].reshape(2, 8, 128).transpose(2, 0, 1).reshape(128, 16)
    prm[:, P_BNW:P_BNW + 32] = inp["b_norm_w"].reshape(2, 16, 128).transpose(2, 0, 1).reshape(128, 32)
    prm[:, P_CW:P_CW + 256] = inp["b_conv_w"].reshape(2, 4, 32, 128).transpose(3, 0, 2, 1).reshape(128, 256)
    for j in range(2):
        prm[0:8, P_G + 0 + j] = inp["a_gate_b"][j, 0:8]
        prm[0:8, P_G + 2 + j] = inp["a_gate_b"][j, 8:16]
        prm[0:16, P_G + 4 + j] = inp["b_dt_bias"][j]
        prm[0:16, P_G + 6 + j] = inp["b_a_log"][j]
    return wts, prm, build_consts()


def make_in_maps(inp, cores=range(NCORES)):
    wts, prm, cst = _prep_shared(inp)
    in_maps = []
    for c in cores:
        sq = c * 16 + np.arange(16)
        xp = inp["x_prompt"][c].reshape(NB, TP, D)
        xs = inp["x_sample"][sq].reshape(NB, TS, D)
        xt = np.concatenate([xp, xs], axis=1)
        xin = xt.reshape(NB * NT, 8, 128).transpose(2, 1, 0)
        C = inp["state_mlstm_C"][:, sq]
        n_ = inp["state_mlstm_n"][:, sq]
        Ca = np.concatenate([C, n_[..., None]], axis=-1)
        Ca = Ca.reshape(2, NB, 4, 4, 2, 64, 129)
        cin = Ca.transpose(0, 1, 3, 4, 5, 2, 6).reshape(2, NB, 4, 128, 4 * 129)
        m_ = inp["state_mlstm_m"][:, sq].reshape(2, NB, 4, 8).transpose(0, 1, 3, 2)
        S = inp["state_gdn_S"][:, sq].reshape(2, NB, 4, 16, 128, 128)
        sin = S.transpose(0, 1, 3, 4, 2, 5).reshape(2, NB, 16, 128, 4 * 128)
        cvs = inp["state_gdn_conv"][:, sq].reshape(2, NB, 4, 3, 32, 128)
        cvin = cvs.transpose(0, 1, 4, 5, 2, 3).reshape(2, NB, 32, 128, 12)
        in_maps.append({
            "xin": np.ascontiguousarray(xin, np.float32), "wts": wts, "cst": cst, "prm": prm,
            "cin": np.ascontiguousarray(cin, np.float32), "min": np.ascontiguousarray(m_, np.float32),
            "sin": np.ascontiguousarray(sin, np.float32), "cvin": np.ascontiguousarray(cvin, np.float32),
        })
    return in_maps


def assemble(R, cores=range(NCORES)):
    y_prompt = np.zeros((8, 2048, D), np.float32)
    y_sample = np.zeros((128, 4, D), np.float32)
    p_C = np.zeros((2, 8, 8, 64, 128), np.float32)
    p_n = np.zeros((2, 8, 8, 64), np.float32)
    p_m = np.zeros((2, 8, 8), np.float32)
    p_S = np.zeros((2, 8, 16, 128, 128), np.float32)
    p_cv = np.zeros((2, 8, 3, 4096), np.float32)
    s_C = np.zeros((2, 128, 8, 64, 128), np.float32)
    s_n = np.zeros((2, 128, 8, 64), np.float32)
    s_m = np.zeros((2, 128, 8), np.float32)
    s_S = np.zeros((2, 128, 16, 128, 128), np.float32)
    s_cv = np.zeros((2, 128, 3, 4096), np.float32)
    for i, c in enumerate(cores):
        r = R[i]
        sq = c * 16 + np.arange(16)
        y = r["yout"].transpose(2, 1, 0).reshape(NB, NT, D)
        y_prompt[c] = y[:, :TP].reshape(2048, D)
        y_sample[sq] = y[:, TP:].reshape(16, 4, D)
        pc = r["pC"].reshape(2, 4, 2, 64, 129).reshape(2, 8, 64, 129)
        p_C[:, c] = pc[..., :128]
        p_n[:, c] = pc[..., 128]
        p_m[:, c] = r["pm"].reshape(2, 8)
        p_S[:, c] = r["pS"].reshape(2, 128, 16, 128).transpose(0, 2, 1, 3)
        p_cv[:, c] = r["pcv"].reshape(2, 128, 32, 3).transpose(0, 3, 2, 1).reshape(2, 3, 4096)
        co = r["cout"].reshape(2, NB, 4, 2, 64, 4, 129).transpose(0, 1, 5, 2, 3, 4, 6).reshape(2, 16, 8, 64, 129)
        s_C[:, sq] = co[..., :128]
        s_n[:, sq] = co[..., 128]
        s_m[:, sq] = r["mout"].transpose(0, 1, 3, 2).reshape(2, 16, 8)
        so = r["sout"].reshape(2, NB, 16, 128, 4, 128).transpose(0, 1, 4, 2, 3, 5).reshape(2, 16, 16, 128, 128)
        s_S[:, sq] = so
        cvo = r["cvout"].reshape(2, NB, 32, 128, 4, 3).transpose(0, 1, 4, 5, 2, 3).reshape(2, 16, 3, 4096)
        s_cv[:, sq] = cvo
    return (y_prompt, y_sample, p_C, p_n, p_m, p_S, p_cv, s_C, s_n, s_m, s_S, s_cv)


def kernel(**inputs):
    inp = {k: np.asarray(v) for k, v in inputs.items()}
    if "nc" not in _CACHE:
        _CACHE["nc"] = build()
    nc = _CACHE["nc"]
    in_maps = make_in_maps(inp)
    res = run_bass_kernel_spmd(nc, in_maps, core_ids=list(range(NCORES)))
    return assemble(res.results)
```
